# Optimizing a Trainium2 kernel written in Bass

```python
import math
import jax, jax.numpy as jnp
from jax import lax
import numpy as np

D_MODEL = 1024
BATCH = 16
SEQ = 2048
DEPTH = 1

N_META = 16
GRID_W = 64
Q_BLOCK = 128
HEAD_DIM = 64
ATTN_WIDTH = D_MODEL // 2
ATTN_HEADS = ATTN_WIDTH // HEAD_DIM
KV_HEADS = 2
KV_WIDTH = KV_HEADS * HEAD_DIM
FOURIER_WIDTH = D_MODEL - ATTN_WIDTH
FOURIER_GROUP_DIM = 64
FOURIER_GROUPS = FOURIER_WIDTH // FOURIER_GROUP_DIM
MIX_WIDTH = ATTN_WIDTH + FOURIER_WIDTH
IN_WIDTH = ATTN_WIDTH + 2 * KV_WIDTH + FOURIER_WIDTH
D_FF = 2816
ROPE_THETA = 10000.0
ROPE_PAIRS = HEAD_DIM // 2
ROPE_AXIS_PAIRS = ROPE_PAIRS // 2
RMS_EPS = 1e-6
LN_EPS = 1e-5
DEEPNORM_ALPHA = (2.0 * DEPTH) ** 0.25
DEEPNORM_BETA = (8.0 * DEPTH) ** -0.25

kernel_name = "hybrid_gqa_fourier_macaron_deepnorm_encoder"


def layer_norm(x, g, b):
    xf = x.astype(jnp.float32)
    mu = jnp.mean(xf, axis=-1, keepdims=True)
    var = jnp.mean(jnp.square(xf - mu), axis=-1, keepdims=True)
    y = (xf - mu) * lax.rsqrt(var + LN_EPS) * g.astype(jnp.float32) + b.astype(jnp.float32)
    return y.astype(x.dtype)


def rms_norm(x, g):
    xf = x.astype(jnp.float32)
    y = xf * lax.rsqrt(jnp.mean(jnp.square(xf), axis=-1, keepdims=True) + RMS_EPS) * g.astype(jnp.float32)
    return y.astype(x.dtype)


def swiglu(x, w_gate, w_up, w_down):
    return (jax.nn.silu(x @ w_gate) * (x @ w_up)) @ w_down


def axial_rope_tables(n_tokens):
    n_rows = n_tokens // GRID_W
    row = jnp.repeat(jnp.arange(n_rows, dtype=jnp.float32), GRID_W)
    col = jnp.tile(jnp.arange(GRID_W, dtype=jnp.float32), n_rows)
    inv_freq = ROPE_THETA ** (-jnp.arange(ROPE_AXIS_PAIRS, dtype=jnp.float32) / ROPE_AXIS_PAIRS)
    ang = jnp.concatenate([row[:, None] * inv_freq, col[:, None] * inv_freq], axis=-1)
    ang = jnp.concatenate([jnp.zeros((N_META, ROPE_PAIRS), jnp.float32), ang], axis=0)
    return jnp.cos(ang), jnp.sin(ang)


def apply_rope(x, cos, sin):
    B, L, H, D = x.shape
    xp = x.astype(jnp.float32).reshape(B, L, H, D // 2, 2)
    x0, x1 = xp[..., 0], xp[..., 1]
    c = cos[None, :, None, :]
    s = sin[None, :, None, :]
    out = jnp.stack([x0 * c - x1 * s, x0 * s + x1 * c], axis=-1)
    return out.reshape(B, L, H, D).astype(x.dtype)


def gqa_attention(q, k, v):
    B, L, H, Dh = q.shape
    G = k.shape[2]
    R = H // G
    q = q.reshape(B, L, G, R, Dh)
    scale = Dh ** -0.5

    def attend(qb):
        s = jnp.einsum('bqgrd,bkgd->bgrqk', qb, k, preferred_element_type=jnp.float32) * scale
        p = jax.nn.softmax(s, axis=-1).astype(v.dtype)
        return jnp.einsum('bgrqk,bkgd->bqgrd', p, v)

    out_meta = attend(q[:, :N_META]).reshape(B, N_META, H * Dh)
    n_real = L - N_META
    n_blocks = n_real // Q_BLOCK
    q_blocks = q[:, N_META:].reshape(B, n_blocks, Q_BLOCK, G, R, Dh).transpose(1, 0, 2, 3, 4, 5)
    out_real = lax.map(attend, q_blocks)
    out_real = out_real.transpose(1, 0, 2, 3, 4, 5).reshape(B, n_real, H * Dh)
    return jnp.concatenate([out_meta, out_real], axis=1)


def fourier_mix(u):
    y = jnp.fft.fft2(u.astype(jnp.float32), axes=(1, 3), norm='ortho').real
    return y.astype(u.dtype)


def token_mixers(h, cos, sin, w_in, q_norm_g, k_norm_g, attn_out_g, fourier_out_g, w_out):
    B, L, _ = h.shape
    z = h @ w_in
    q, k, v, f = jnp.split(z, [ATTN_WIDTH, ATTN_WIDTH + KV_WIDTH, ATTN_WIDTH + 2 * KV_WIDTH], axis=-1)
    q = apply_rope(rms_norm(q.reshape(B, L, ATTN_HEADS, HEAD_DIM), q_norm_g), cos, sin)
    k = apply_rope(rms_norm(k.reshape(B, L, KV_HEADS, HEAD_DIM), k_norm_g), cos, sin)
    v = v.reshape(B, L, KV_HEADS, HEAD_DIM)
    attn = gqa_attention(q, k, v)
    four = fourier_mix(f.reshape(B, L, FOURIER_GROUPS, FOURIER_GROUP_DIM)).reshape(B, L, FOURIER_WIDTH)
    merged = jnp.concatenate([rms_norm(attn, attn_out_g), rms_norm(four, fourier_out_g)], axis=-1)
    return merged @ w_out


def hybrid_layer(h, cos, sin,
                 ff1_gate, ff1_up, ff1_down, ln1_g, ln1_b,
                 w_in, q_norm_g, k_norm_g, attn_out_g, fourier_out_g, w_out, ln2_g, ln2_b,
                 ff2_gate, ff2_up, ff2_down, ln3_g, ln3_b):
    a = DEEPNORM_ALPHA
    h = layer_norm(a * h + 0.5 * swiglu(h, ff1_gate, ff1_up, ff1_down), ln1_g, ln1_b)
    mix = token_mixers(h, cos, sin, w_in, q_norm_g, k_norm_g, attn_out_g, fourier_out_g, w_out)
    h = layer_norm(a * h + mix, ln2_g, ln2_b)
    h = layer_norm(a * h + 0.5 * swiglu(h, ff2_gate, ff2_up, ff2_down), ln3_g, ln3_b)
    return h


def setup_inputs(seed: int = 0) -> dict:
    key = jax.random.key(seed)
    ks = jax.random.split(key, 24)
    nrm = lambda k, shape: jax.random.normal(k, shape, jnp.float32)
    gain = lambda k, shape: 1.0 + 0.02 * nrm(k, shape)
    bias = lambda k, shape: 0.02 * nrm(k, shape)
    beta = DEEPNORM_BETA
    return {
        "x": nrm(ks[0], (BATCH, SEQ, D_MODEL)),
        "meta_tokens": nrm(ks[1], (N_META, D_MODEL)),
        "ln_emb_g": gain(ks[2], (D_MODEL,)),
        "ln_emb_b": bias(ks[3], (D_MODEL,)),
        "ff1_gate": nrm(ks[4], (DEPTH, D_MODEL, D_FF)) * D_MODEL ** -0.5,
        "ff1_up": nrm(ks[5], (DEPTH, D_MODEL, D_FF)) * D_MODEL ** -0.5,
        "ff1_down": nrm(ks[6], (DEPTH, D_FF, D_MODEL)) * (D_FF ** -0.5 * beta),
        "ln1_g": gain(ks[7], (DEPTH, D_MODEL)),
        "ln1_b": bias(ks[8], (DEPTH, D_MODEL)),
        "w_in": nrm(ks[9], (DEPTH, D_MODEL, IN_WIDTH)) * D_MODEL ** -0.5,
        "q_norm_g": gain(ks[10], (DEPTH, HEAD_DIM)),
        "k_norm_g": gain(ks[11], (DEPTH, HEAD_DIM)),
        "attn_out_g": gain(ks[12], (DEPTH, ATTN_WIDTH)),
        "fourier_out_g": gain(ks[13], (DEPTH, FOURIER_WIDTH)),
        "w_out": nrm(ks[14], (DEPTH, MIX_WIDTH, D_MODEL)) * (MIX_WIDTH ** -0.5 * beta),
        "ln2_g": gain(ks[15], (DEPTH, D_MODEL)),
        "ln2_b": bias(ks[16], (DEPTH, D_MODEL)),
        "ff2_gate": nrm(ks[17], (DEPTH, D_MODEL, D_FF)) * D_MODEL ** -0.5,
        "ff2_up": nrm(ks[18], (DEPTH, D_MODEL, D_FF)) * D_MODEL ** -0.5,
        "ff2_down": nrm(ks[19], (DEPTH, D_FF, D_MODEL)) * (D_FF ** -0.5 * beta),
        "ln3_g": gain(ks[20], (DEPTH, D_MODEL)),
        "ln3_b": bias(ks[21], (DEPTH, D_MODEL)),
    }


def reference(x, meta_tokens, ln_emb_g, ln_emb_b,
              ff1_gate, ff1_up, ff1_down, ln1_g, ln1_b,
              w_in, q_norm_g, k_norm_g, attn_out_g, fourier_out_g, w_out, ln2_g, ln2_b,
              ff2_gate, ff2_up, ff2_down, ln3_g, ln3_b):
    B, S, D = x.shape
    meta = jnp.broadcast_to(meta_tokens.astype(x.dtype)[None], (B, N_META, D))
    h = jnp.concatenate([meta, x], axis=1)
    h = layer_norm(h, ln_emb_g, ln_emb_b)
    cos, sin = axial_rope_tables(S)
    for l in range(DEPTH):
        h = hybrid_layer(h, cos, sin,
                         ff1_gate[l], ff1_up[l], ff1_down[l], ln1_g[l], ln1_b[l],
                         w_in[l], q_norm_g[l], k_norm_g[l], attn_out_g[l], fourier_out_g[l], w_out[l],
                         ln2_g[l], ln2_b[l],
                         ff2_gate[l], ff2_up[l], ff2_down[l], ln3_g[l], ln3_b[l])
    return h[:, N_META:]
```

```python
import contextlib
import numpy as np
import ml_dtypes
import concourse.bass as bass
import concourse.mybir as mybir
from concourse.bass_utils import run_bass_kernel_spmd

F32 = mybir.dt.float32
BF16 = mybir.dt.bfloat16
AF = mybir.ActivationFunctionType
ALU = mybir.AluOpType

ENGS = ("sync", "act", "pe", "dve", "pool")


class SemSlot:
    __slots__ = ("dsem", "dcount")

    def __init__(self):
        self.dsem = None
        self.dcount = 0


class Buf:
    __slots__ = ("name", "w", "r", "slot", "excl")

    def __init__(self, name, excl=False, slot=None):
        self.name = name
        self.w = None
        self.r = []
        self.slot = slot if slot is not None else {}
        self.excl = excl


class Op:
    __slots__ = ("eng", "fn", "deps", "signal", "idx", "token", "is_dma", "seq", "cost", "succ", "nd", "ready", "fin",
                 "tag", "st", "crit")

    def __init__(self, eng, fn):
        self.eng = eng
        self.fn = fn
        self.deps = {}
        self.signal = False
        self.idx = 0
        self.token = None
        self.is_dma = False
        self.seq = 0
        self.cost = 100.0
        self.succ = []
        self.nd = 0
        self.ready = 0.0
        self.fin = 0.0
        self.tag = ""
        self.st = 0.0
        self.crit = None


STRICT_SAME_ENGINE = True
DMA_BW = 180.0
SEM_LAT = 250.0
PRIO_CP = True


class Prog:
    def __init__(self, nc):
        self.nc = nc
        self.ops = {e: [] for e in ENGS}
        self.all_dma_bufs = []
        self.nseq = 0
        self.tag = ""

    def _track(self, op, reads, writes):
        xr = [b for b in reads if b.excl and b not in writes]
        if xr:
            writes = list(writes) + xr
            reads = [b for b in reads if not b.excl]
        for b in reads:
            if b.w is not None:
                op.deps[b.w] = True
            b.r.append(op)
        for b in writes:
            if b.w is not None:
                op.deps.setdefault(b.w, False)
            for r in b.r:
                if r is not op:
                    op.deps.setdefault(r, False)
            b.w = op
            b.r = []

    def op(self, eng, fn, reads=(), writes=(), cost=100.0):
        o = Op(eng, fn)
        o.cost = cost
        o.tag = self.tag
        o.seq = self.nseq
        self.nseq += 1
        self._track(o, reads, writes)
        self.ops[eng].append(o)
        return o

    def dma(self, out, in_, reads=(), writes=(), sembuf=None, queue="sync", nbytes=0):
        o = Op(queue, None)
        o.is_dma = True
        o.tag = self.tag
        o.cost = float(nbytes)
        o.seq = self.nseq
        self.nseq += 1
        self._track(o, reads, writes)
        kind = "sw" if queue == "pool" else "hw"
        if kind not in sembuf.slot:
            sembuf.slot[kind] = SemSlot()
        slot = sembuf.slot[kind]
        if slot.dsem is None:
            self.all_dma_bufs.append(slot)
            slot.dsem = "pending"
        slot.dcount += 16
        o.token = (slot, slot.dcount)
        o.fn = (out, in_)
        self.ops[queue].append(o)
        return o

    @staticmethod
    def _skip(d, o, raw):
        if d.is_dma or d.eng != o.eng:
            return False
        if o.is_dma:
            return False
        if o.eng == "pe":
            return True
        return (not raw) and (not STRICT_SAME_ENGINE)

    def schedule(self):
        import heapq
        allops = [o for e in ENGS for o in self.ops[e]]
        for o in allops:
            o.succ = []
            o.ready = 0.0
        for o in allops:
            o.nd = len(o.deps)
            for d in o.deps:
                d.succ.append(o)
        byseq = sorted(allops, key=lambda o: o.seq)
        bl = {}
        for o in reversed(byseq):
            c = (o.cost / DMA_BW + 2000.0) if o.is_dma else o.cost
            m = 0.0
            for sc in o.succ:
                v = bl[sc]
                if v > m:
                    m = v
            bl[o] = c + m
        if PRIO_CP:
            for o in allops:
                o.seq = -bl[o] + o.seq * 1e-6
        later = {e: [] for e in ENGS}
        now = {e: [] for e in ENGS}
        free = {e: 0.0 for e in ENGS}
        for o in allops:
            if o.nd == 0:
                heapq.heappush(later[o.eng], (0.0, o.seq, o))
        new = {e: [] for e in ENGS}
        dma_free = 0.0
        left = len(allops)
        while left:
            best = None
            for e in ENGS:
                T = free[e]
                lt, nw = later[e], now[e]
                while lt and lt[0][0] <= T:
                    r, q, o = heapq.heappop(lt)
                    heapq.heappush(nw, (q, o))
                if nw:
                    st = T
                elif lt:
                    st = lt[0][0]
                else:
                    continue
                if best is None or st < best[0]:
                    best = (st, e)
            st, e = best
            if now[e]:
                q, o = heapq.heappop(now[e])
            else:
                r, q, o = heapq.heappop(later[e])
            if o.is_dma:
                free[e] = st + 80.0
                d0 = max(st + 80.0, dma_free)
                dma_free = d0 + o.cost / DMA_BW
                o.fin = dma_free + 2000.0
            else:
                o.fin = st + o.cost
                free[e] = o.fin
            new[e].append(o)
            o.st = st
            left -= 1
            for sc in o.succ:
                lat = 0.0 if (sc.eng == o.eng and not o.is_dma) else SEM_LAT
                if o.fin + lat > sc.ready:
                    sc.ready = o.fin + lat
                    sc.crit = o
                sc.nd -= 1
                if sc.nd == 0:
                    heapq.heappush(later[sc.eng], (sc.ready, sc.seq, sc))
        self.ops = new
        self.makespan = max(free.values())

    def finalize(self, final_waits=(), do_schedule=True):
        nc = self.nc
        if do_schedule:
            self.schedule()
        pos = {}
        for e in ENGS:
            for i, o in enumerate(self.ops[e]):
                pos[o] = i + 1
        wl = {}
        for e in ENGS:
            wpos = {}
            wdma = {}
            for o in self.ops[e]:
                lst = []
                for d, raw in o.deps.items():
                    if d.is_dma:
                        slot, val = d.token
                        if wdma.get(id(slot), 0) >= val:
                            continue
                        wdma[id(slot)] = val
                        lst.append(d)
                    else:
                        if self._skip(d, o, raw):
                            continue
                        if wpos.get(d.eng, 0) >= pos[d]:
                            continue
                        lst.append(d)
                best = {}
                out = []
                for d in lst:
                    if d.is_dma:
                        out.append(d)
                    elif d.eng not in best or pos[d] > pos[best[d.eng]]:
                        best[d.eng] = d
                for f, d in best.items():
                    wpos[f] = pos[d]
                    d.signal = True
                    out.append(d)
                wl[o] = out
        for e in ENGS:
            n = 0
            for o in self.ops[e]:
                if o.signal and not o.is_dma:
                    n += 1
                    o.idx = n
        with contextlib.ExitStack() as st:
            esem = {e: st.enter_context(nc.semaphore("es_" + e)) for e in ENGS}
            for i, b in enumerate(self.all_dma_bufs):
                b.dsem = st.enter_context(nc.semaphore("ds%d" % i))
            block = st.enter_context(nc.Block())

            def emit(engname, eng):
                for o in self.ops[engname]:
                    for d in wl[o]:
                        if d.is_dma:
                            eng.wait_ge(d.token[0].dsem, d.token[1])
                        else:
                            eng.wait_ge(esem[d.eng], d.idx)
                    if o.is_dma:
                        out, in_ = o.fn
                        eng.dma_start(out=out, in_=in_).then_inc(o.token[0].dsem, 16)
                    else:
                        ins = o.fn(eng)
                        if o.signal:
                            ins.then_inc(esem[engname], 1)
                if engname == "sync":
                    for d in final_waits:
                        eng.wait_ge(d.token[0].dsem, d.token[1])

            @block.sync
            def _(eng):
                emit("sync", eng)

            @block.scalar
            def _(eng):
                emit("act", eng)

            @block.tensor
            def _(eng):
                emit("pe", eng)

            @block.vector
            def _(eng):
                emit("dve", eng)

            @block.gpsimd
            def _(eng):
                emit("pool", eng)


class Tile:
    __slots__ = ("ap", "buf", "start", "end", "cb")

    def __init__(self, ap, buf, start, end):
        self.ap = ap
        self.buf = buf
        self.start = start
        self.end = end
        self.cb = None

    def chunked(self, n):
        if self.buf.name.startswith("h16"):
            self.cb = [self.buf] * n
            return self
        self.cb = [Buf(self.buf.name + "_c%d" % c, slot=self.buf.slot) for c in range(n)]
        for b in self.cb:
            b.r = list(self.buf.r)
        return self


_DTSIZE = {F32: 4, BF16: 2}


class Arena:
    def __init__(self, tensor, nbytes):
        self.t = tensor
        self.nbytes = nbytes
        self.live = []
        self.dead = []
        self.slots = {}
        self.lo = 0
        self.hi = nbytes
        self.peak = 0

    def tile(self, name, off, dtype, shape):
        n = 1
        for s in shape:
            n *= s
        nb = n * _DTSIZE[dtype]
        assert off % 4 == 0 and nb % 4 == 0, (name, off, nb)
        end = off + nb
        assert end <= self.nbytes, (name, end, self.nbytes)
        for t in self.live:
            assert end <= t.start or off >= t.end, ("overlap", name, t.buf.name)
        if off not in self.slots:
            self.slots[off] = {}
        b = Buf(name, slot=self.slots[off])
        keep = []
        for (s, e, ops) in self.dead:
            if not (end <= s or off >= e):
                for o in ops:
                    if o not in b.r:
                        b.r.append(o)
            keep.append((s, e, ops))
        ap = self.t[:, off // 4: end // 4]
        if dtype != F32:
            ap = ap.bitcast(dtype)
        if len(shape) == 2:
            ap = ap.rearrange("p (a b) -> p a b", a=shape[0])
        elif len(shape) == 3:
            ap = ap.rearrange("p (a b c) -> p a b c", a=shape[0], b=shape[1])
        tl = Tile(ap, b, off, end)
        self.live.append(tl)
        return tl

    def low(self, name, dtype, shape):
        t = self.tile(name, self.lo, dtype, shape)
        self.lo = (t.end + 63) // 64 * 64
        assert self.lo <= self.hi, ("arena full", name, self.lo, self.hi)
        self.peak = max(self.peak, self.lo + (self.nbytes - self.hi))
        return t

    def high(self, name, dtype, shape):
        n = 1
        for s in shape:
            n *= s
        nb = (n * _DTSIZE[dtype] + 63) // 64 * 64
        off = self.hi - nb
        assert off >= self.lo, ("arena full", name, self.lo, off)
        t = self.tile(name, off, dtype, shape)
        self.hi = off
        self.peak = max(self.peak, self.lo + (self.nbytes - self.hi))
        return t

    def region(self, start, end):
        return Region(self, start, end)

    def free(self, *tiles):
        for tl in tiles:
            self.live.remove(tl)
            ops = list(tl.buf.r)
            if tl.buf.w is not None:
                ops.append(tl.buf.w)
            for b in (tl.cb or ()):
                ops.extend(b.r)
                if b.w is not None:
                    ops.append(b.w)
            self.dead.append((tl.start, tl.end, ops))


class Region:
    def __init__(self, arena, start, end):
        self.A = arena
        self.lo = start
        self.end = end
        self.tiles = []

    def low(self, name, dtype, shape):
        t = self.A.tile(name, self.lo, dtype, shape)
        self.lo = (t.end + 63) // 64 * 64
        assert self.lo <= self.end, ("region full", name, self.lo, self.end)
        self.tiles.append(t)
        return t

    def free_all(self):
        self.A.free(*self.tiles)
        self.tiles = []


D = 1024
NCH = 8
SEQ = 2048
NMETA = 16
LTOT = SEQ + NMETA
DFF = 2816
NFC = DFF // 128
GS = 4
GROUPS = [(0, 2), (2, 4), (6, 4), (10, 4), (14, 4), (18, 4)]
NGRP = len(GROUPS)
INW = 1280
ALPHA = 2.0 ** 0.25
C_FFN = 0.5 / ALPHA
C_MIX = 1.0 / ALPHA
EPS_LN = 1e-5
EPS_LNS = 1e-5 / (ALPHA * ALPHA)
EPS_RMS = 1e-6
NBLK = 5
NKT = 17
ARENA_BYTES = 207 * 1024
NPAR = 80
PC_EMB_G, PC_EMB_B, PC_LN1_G, PC_LN1_B, PC_LN2_G, PC_LN2_B, PC_LN3_G, PC_LN3_B = 0, 8, 16, 24, 32, 40, 48, 56
PC_QG, PC_KG, PC_OG = 64, 65, 66
CM_ONES1024, CM_BD64, CM_ONES512, CM_ROT, CM_BDC, CM_BDSN = 0, 1, 2, 3, 4, 5


def blkN(blk):
    return 512 if blk < 4 else NMETA


class Builder:
    def __init__(self, npass, stage):
        self.npass = npass
        self.stage = stage
        self.nc = bass.Bass("TRN2", target_bir_lowering=False)
        nc = self.nc

        def din(name, shape, dt):
            return nc.dram_tensor(name, shape, dt, kind="ExternalInput").ap()

        self.d_x = din("x", [npass, SEQ, D], F32)
        self.d_meta = din("meta", [NMETA, D], F32)
        self.d_params = din("params", [128, NPAR], F32)
        self.d_ident = din("ident", [128, 128], F32)
        self.d_cmats = din("cmats", [128, 6 * 128], BF16)
        self.d_rope = din("rope", [128, 2 * LTOT], F32)
        self.d_dftc = din("dftc", [8, 128, NKT * 256], BF16)
        self.d_dfts = din("dfts", [8, 128, NKT * 256], BF16)
        self.d_w = {}
        for nm, shp in (("ff1_gate", [D, DFF]), ("ff1_up", [D, DFF]), ("ff1_down", [DFF, D]),
                        ("ff2_gate", [D, DFF]), ("ff2_up", [D, DFF]), ("ff2_down", [DFF, D]),
                        ("w_in", [D, INW]), ("w_out", [D, D])):
            self.d_w[nm] = din(nm, shp, F32)
        self.d_out = nc.dram_tensor("out", [npass, SEQ, D], F32, kind="ExternalOutput").ap()

    @staticmethod
    def _n(ap):
        n = 1
        for v in ap.shape[1:]:
            n *= v
        return n

    def mm(self, out, lhsT, rhs, start, stop, reads, writes):
        cost = max(self._n(rhs), 64) / 2.4 + 4.0
        self.P.op("pe", lambda e: e.matmul(out, lhsT=lhsT, rhs=rhs, start=start, stop=stop), reads, writes, cost)

    def tr(self, out, in_, ident, reads, writes):
        self.P.op("pe", lambda e: e.transpose(out, in_, ident), reads, writes, 180.0)

    def act(self, out, in_, func, reads, writes, scale=1.0, bias=None):
        cost = (self._n(out) + 100) / 1.2 + (0.0 if isinstance(scale, float) else 93.0)
        if bias is None:
            self.P.op("act", lambda e: e.activation(out=out, in_=in_, func=func, scale=scale), reads, writes, cost)
        else:
            self.P.op("act", lambda e: e.activation(out=out, in_=in_, func=func, scale=scale, bias=bias), reads, writes,
                      cost + 93.0)

    def tt(self, eng, out, in0, in1, op, reads, writes):
        n = self._n(out)
        cost = n / 0.96 + 160.0 if eng == "dve" else n * 2.4 + 100.0
        self.P.op(eng, lambda e: e.tensor_tensor(out=out, in0=in0, in1=in1, op=op), reads, writes, cost)

    def stt(self, out, in0, scalar, in1, op0, op1, reads, writes):
        self.P.op("dve", lambda e: e.scalar_tensor_tensor(out=out, in0=in0, scalar=scalar, in1=in1, op0=op0, op1=op1),
                  reads, writes, self._n(out) / 0.96 + 160.0)

    def cp(self, eng, out, in_, reads, writes):
        n = self._n(out)
        cost = n / 0.96 + 160.0 if eng == "dve" else n * 3.6 + 100.0
        self.P.op(eng, lambda e: e.tensor_copy(out=out, in_=in_), reads, writes, cost)

    def memset(self, eng, ap, val, writes):
        self.P.op(eng, lambda e: e.memset(ap, val), (), writes, 200.0)

    def recip(self, out, in_, reads, writes):
        self.P.op("dve", lambda e: e.reciprocal(out=out, in_=in_), reads, writes, self._n(out) * 6.5 + 60.0)

    def dma(self, out, in_, reads=(), writes=(), sembuf=None, queue="sync"):
        n = in_.shape[0] * self._n(in_)
        nbytes = n * (4 if in_.dtype == F32 else 2)
        return self.P.dma(out, in_, reads=reads, writes=writes, sembuf=sembuf, queue=queue, nbytes=nbytes)

    def build(self):
        nc = self.nc
        with contextlib.ExitStack() as st:
            arena_t = st.enter_context(nc.sbuf_tensor("arena", [128, ARENA_BYTES // 4], F32))
            pst = [st.enter_context(nc.psum_tensor("ps%d" % i, [128, 512], F32)) for i in range(8)]
            self.ps = [Tile(pst[i][:, :], Buf("ps%d" % i, excl=True), 0, 0) for i in range(8)]
            self.A = Arena(arena_t, ARENA_BYTES)
            self.P = Prog(nc)
            self.out_dmas = []
            self.setup_constants()
            for b in range(self.npass):
                self.run_pass(b)
            self.P.finalize(final_waits=self.out_dmas)
        return nc

    def setup_constants(self):
        A, P = self.A, self.P
        self.params = A.low("params", F32, [NPAR])
        self.ident = A.low("ident", F32, [128])
        self.cm = A.low("cmats", BF16, [6, 128])
        self.epsT = A.low("eps", F32, [4])
        self.dma(self.params.ap, self.d_params, writes=[self.params.buf], sembuf=self.params.buf)
        self.dma(self.ident.ap, self.d_ident, writes=[self.ident.buf], sembuf=self.ident.buf)
        self.dma(self.cm.ap, self.d_cmats.rearrange("p (a b) -> p a b", a=6), writes=[self.cm.buf], sembuf=self.cm.buf)
        self.memset("dve", self.epsT.ap[:, 0:1], EPS_LN, [self.epsT.buf])
        self.memset("dve", self.epsT.ap[:, 1:2], EPS_LNS, [self.epsT.buf])
        self.memset("dve", self.epsT.ap[:, 2:3], EPS_RMS, [self.epsT.buf])
        self.h32 = [A.low("h32_%d" % k, F32, [NCH, blkN(k)]).chunked(NCH) for k in range(NBLK)]
        self.h16 = [A.low("h16_%d" % k, BF16, [NCH, blkN(k)]).chunked(NCH) for k in range(NBLK)]
        self.kmeta = A.low("kmeta", BF16, [4, NMETA])
        self.vmeta = A.low("vmeta", BF16, [2, 128])
        self.fmeta = A.low("fmeta", BF16, [512])
        self.base_lo = A.lo

    def cmat(self, k):
        return self.cm.ap[:, k, :]

    def ln_temps(self, nt=3):
        A = self.A
        t = {}
        t["sq16"] = [A.low("ln_sq16_%d" % c, BF16, [512]) for c in range(NCH)]
        sqt = t["sq16"]
        t["sq16c"] = lambda c: (sqt[c].ap, sqt[c].buf)
        t["mean"] = [A.low("ln_mean%d" % i, F32, [512]) for i in range(2)]
        t["var"] = [A.low("ln_var%d" % i, F32, [512]) for i in range(2)]
        t["rstd"] = [A.low("ln_rstd%d" % i, F32, [512]) for i in range(2)]
        t["t"] = [A.low("ln_t%d" % i, F32, [512]) for i in range(nt)]
        t["all"] = t["sq16"] + t["mean"] + t["var"] + t["rstd"] + t["t"]
        return t

    def layer_norm(self, blk, gcol, bcol, epscol, lt, make_h16=True, bank_a=6, bank_b=7, s16c=None):
        N = blkN(blk)
        s32 = self.h32[blk]
        if s16c is None:
            h16t = self.h16[blk]
            s16c = lambda c: (h16t.ap[:, c, :N], h16t.cb[c])
        sq16c = lt["sq16c"]
        pa, pb = self.ps[bank_a], self.ps[bank_b]
        ones = self.cmat(CM_ONES1024)
        for c in range(NCH):
            a16_, b16_ = s16c(c)
            self.cp("dve", a16_[:, :N] if N < 512 else a16_, s32.ap[:, c, :N], [s32.cb[c]], [b16_])
            q16_, qb_ = sq16c(c)
            self.act(q16_[:, :N] if N < 512 else q16_, s32.ap[:, c, :N], AF.Square, [s32.cb[c]], [qb_])
        for c in range(NCH):
            a16_, b16_ = s16c(c)
            self.mm(pa.ap[:, :N], ones, a16_[:, :N] if N < 512 else a16_, c == 0, c == NCH - 1, [self.cm.buf, b16_], [pa.buf])
        for c in range(NCH):
            q16_, qb_ = sq16c(c)
            self.mm(pb.ap[:, :N], ones, q16_[:, :N] if N < 512 else q16_, c == 0, c == NCH - 1, [self.cm.buf, qb_], [pb.buf])
        self.ln_count = getattr(self, "ln_count", 0) + 1
        k = self.ln_count % len(lt["mean"])
        mean, var, rstd = lt["mean"][k], lt["var"][k], lt["rstd"][k]
        self.act(mean.ap[:, :N], pa.ap[:, :N], AF.Copy, [pa.buf], [mean.buf])
        self.tt("dve", var.ap[:, :N], mean.ap[:, :N], mean.ap[:, :N], ALU.mult, [mean.buf], [var.buf])
        self.tt("dve", var.ap[:, :N], pb.ap[:, :N], var.ap[:, :N], ALU.subtract, [pb.buf, var.buf], [var.buf])
        self.act(var.ap[:, :N], var.ap[:, :N], AF.Ln, [var.buf, self.epsT.buf], [var.buf],
                 bias=self.epsT.ap[:, epscol:epscol + 1])
        self.act(rstd.ap[:, :N], var.ap[:, :N], AF.Exp, [var.buf], [rstd.buf], scale=-0.5)
        nt = len(lt["t"])
        for c in range(NCH):
            t = lt["t"][c % nt]
            self.tt("dve", t.ap[:, :N], s32.ap[:, c, :N], mean.ap[:, :N], ALU.subtract, [s32.cb[c], mean.buf], [t.buf])
            self.tt("dve", t.ap[:, :N], t.ap[:, :N], rstd.ap[:, :N], ALU.mult, [t.buf, rstd.buf], [t.buf])
            self.act(s32.ap[:, c, :N], t.ap[:, :N], AF.Identity, [t.buf, self.params.buf], [s32.cb[c]],
                     scale=self.params.ap[:, gcol + c:gcol + c + 1], bias=self.params.ap[:, bcol + c:bcol + c + 1])
            if make_h16:
                self.act(self.h16[blk].ap[:, c, :N], t.ap[:, :N], AF.Identity, [t.buf, self.params.buf],
                         [self.h16[blk].cb[c]],
                         scale=self.params.ap[:, gcol + c:gcol + c + 1], bias=self.params.ap[:, bcol + c:bcol + c + 1])

    def phase_input(self, b):
        self.P.tag = "input"
        A = self.A
        mark = A.lo
        NX = 5
        xin = [A.low("xin%d" % i, F32, [D]) for i in range(NX)]
        lt = self.ln_temps()
        mine = xin + lt["all"]
        k = 0
        for blk in range(NBLK if b == 0 else 4):
            ntile = 4 if blk < 4 else 1
            for j in range(ntile):
                xt = xin[k % NX]
                k += 1
                R = 128 if blk < 4 else NMETA
                if blk < 4:
                    src = self.d_x[b, blk * 512 + j * 128: blk * 512 + (j + 1) * 128, :]
                else:
                    src = self.d_meta
                self.dma(xt.ap[0:R, :], src, writes=[xt.buf], sembuf=xt.buf)
                for half in range(2):
                    pt = self.ps[4 + half]
                    for cc in range(4):
                        c = half * 4 + cc
                        self.tr(pt.ap[:, cc * 128: cc * 128 + R], xt.ap[0:R, c * 128:(c + 1) * 128],
                                self.ident.ap[0:R, 0:R], [xt.buf, self.ident.buf], [pt.buf])
                    dst = self.h32[blk].ap[:, half * 4:(half + 1) * 4, j * 128: j * 128 + R]
                    srcp = pt.ap.rearrange("p (a b) -> p a b", a=4)[:, :, 0:R]
                    eng = "act" if half == 0 else "dve"
                    if eng == "act":
                        self.act(dst, srcp, AF.Copy, [pt.buf], self.h32[blk].cb[half * 4:(half + 1) * 4])
                    else:
                        self.cp("dve", dst, srcp, [pt.buf], self.h32[blk].cb[half * 4:(half + 1) * 4])
            self.layer_norm(blk, PC_EMB_G, PC_EMB_B, 0, lt)
        A.free(*mine)
        A.lo = mark

    def alloc_ffn_weights(self):
        A = self.A
        NS = 2
        wg16, wu16, wd16 = [], [], []
        for i in range(NS):
            wg16.append([A.low("wg16_%d_%d" % (i, j), BF16, [NCH, 128]) for j in range(GS)])
            wu16.append([A.low("wu16_%d_%d" % (i, j), BF16, [NCH, 128]) for j in range(GS)])
            wd16.append([A.low("wd16_%d_%d" % (i, j), BF16, [D]) for j in range(GS)])
        return wg16, wu16, wd16

    def phase_ffn(self, b, which, gcol, bcol, do_output, prefetch=None, weights=None, mark=None):
        self.P.tag = which
        A = self.A
        if mark is None:
            mark = A.lo
        wg_d, wu_d, wd_d = self.d_w[which + "_gate"], self.d_w[which + "_up"], self.d_w[which + "_down"]
        NS = 2
        wg16, wu16, wd16 = weights if weights is not None else self.alloc_ffn_weights()
        NA = 3 if do_output else 2
        a16 = [[A.low("a16_%d_%d" % (i, j), BF16, [512]) for j in range(GS)] for i in range(NA)]
        sg = [A.low("sg%d" % i, F32, [512]) for i in range(NA)]
        lt = self.ln_temps(nt=3 if do_output else 2)
        ostage = [A.low("ost%d" % i, F32, [D]) for i in range(2)] if do_output else []
        mine = [t for l3 in (wg16, wu16, wd16) for l2 in l3 for t in l2] + [t for st_ in a16 for t in st_] + \
            sg + ostage + lt["all"]

        def load_group(grp, slot):
            fc0, gs = GROUPS[grp]
            for j in range(gs):
                f0 = (fc0 + j) * 128
                self.dma(wg16[slot][j].ap, wg_d[:, f0:f0 + 128].rearrange("(kc p) f -> p kc f", p=128),
                         writes=[wg16[slot][j].buf], sembuf=wg16[slot][j].buf, queue="pool")
                self.dma(wu16[slot][j].ap, wu_d[:, f0:f0 + 128].rearrange("(kc p) f -> p kc f", p=128),
                         writes=[wu16[slot][j].buf], sembuf=wu16[slot][j].buf, queue="pool")
            for j in range(gs):
                f0 = (fc0 + j) * 128
                self.dma(wd16[slot][j].ap, wd_d[f0:f0 + 128, :],
                         writes=[wd16[slot][j].buf], sembuf=wd16[slot][j].buf, queue="pool")

        load_group(0, 0)
        nblk = 4 if (do_output or b > 0) else NBLK
        if getattr(self, "h16_stale", False):
            for blk in range(4):
                for c in range(NCH):
                    if c % 2:
                        self.act(self.h16[blk].ap[:, c, :], self.h32[blk].ap[:, c, :], AF.Copy,
                                 [self.h32[blk].cb[c]], [self.h16[blk].cb[c]])
                    else:
                        self.cp("dve", self.h16[blk].ap[:, c, :], self.h32[blk].ap[:, c, :],
                                [self.h32[blk].cb[c]], [self.h16[blk].cb[c]])
            self.h16_stale = False
        cnt = 0
        ycnt = 0
        gj = 0
        for grp in range(NGRP):
            slot = grp % NS
            gs = GROUPS[grp][1]
            if grp + 1 < NGRP:
                load_group(grp + 1, (grp + 1) % NS)
            if prefetch is not None and grp == NGRP - 2:
                prefetch()
            for blk in range(nblk):
                N = blkN(blk)
                h16 = self.h16[blk]
                s32 = self.h32[blk]
                aset = a16[cnt % NA]
                for j in range(gs):
                    gp = self.ps[0 + gj % 2]
                    up = self.ps[2 + gj % 2]
                    for kc in range(NCH):
                        self.mm(gp.ap[:, :N], wg16[slot][j].ap[:, kc, :], h16.ap[:, kc, :N],
                                kc == 0, kc == NCH - 1, [wg16[slot][j].buf, h16.cb[kc]], [gp.buf])
                    for kc in range(NCH):
                        self.mm(up.ap[:, :N], wu16[slot][j].ap[:, kc, :], h16.ap[:, kc, :N],
                                kc == 0, kc == NCH - 1, [wu16[slot][j].buf, h16.cb[kc]], [up.buf])
                    sgt = sg[gj % NA]
                    gj += 1
                    self.act(sgt.ap[:, :N], gp.ap[:, :N], AF.Silu, [gp.buf], [sgt.buf])
                    self.tt("dve", aset[j].ap[:, :N], sgt.ap[:, :N], up.ap[:, :N], ALU.mult,
                            [sgt.buf, up.buf], [aset[j].buf])
                for dc in range(NCH):
                    yp = self.ps[4 + ycnt % 2]
                    ycnt += 1
                    for j in range(gs):
                        self.mm(yp.ap[:, :N], wd16[slot][j].ap[:, dc * 128:(dc + 1) * 128], aset[j].ap[:, :N],
                                j == 0, j == gs - 1, [wd16[slot][j].buf, aset[j].buf], [yp.buf])
                    self.stt(s32.ap[:, dc, :N], yp.ap[:, :N], C_FFN, s32.ap[:, dc, :N], ALU.mult, ALU.add,
                             [yp.buf, s32.cb[dc]], [s32.cb[dc]])
                cnt += 1
                if grp == NGRP - 1:
                    self.P.tag = which + ".ln"
                    if blk == nblk - 1:
                        self.layer_norm(blk, gcol, bcol, 1, lt, make_h16=not do_output, bank_a=0, bank_b=2)
                    else:
                        self.layer_norm(blk, gcol, bcol, 1, lt, make_h16=not do_output)
                    if do_output:
                        self.P.tag = which + ".out"
                        self.output_block(b, blk, ostage)
                    self.P.tag = which
        A.free(*mine)
        A.lo = mark

    def output_block(self, b, blk, ostage):
        for j in range(4):
            ot = ostage[j % 2]
            for half in range(2):
                pt = self.ps[6 + half]
                for cc in range(4):
                    c = half * 4 + cc
                    self.tr(pt.ap[:, cc * 128:(cc + 1) * 128], self.h32[blk].ap[:, c, j * 128:(j + 1) * 128],
                            self.ident.ap, [self.h32[blk].cb[c], self.ident.buf], [pt.buf])
                if half == 0:
                    self.act(ot.ap[:, 0:512], pt.ap, AF.Copy, [pt.buf], [ot.buf])
                else:
                    self.cp("dve", ot.ap[:, 512:1024], pt.ap, [pt.buf], [ot.buf])
            d = self.dma(self.d_out[b, blk * 512 + j * 128: blk * 512 + (j + 1) * 128, :], ot.ap,
                           reads=[ot.buf], sembuf=ot.buf)
            self.out_dmas.append(d)

    KB16 = 4 * LTOT * 2
    CARRY = 16384 + KB16 + 8704 + 17408

    def alloc_win(self):
        A = self.A
        off = A.nbytes - NCH * 1408 * 2
        self.win16 = A.tile("win16", off, BF16, [NCH, 1408])
        self.carry0 = off - self.CARRY
        A.hi = off

    def load_win(self):
        w = self.d_w["w_in"]
        win = self.win16
        pieces = [(0, 0, 512), (512, 512, 64), (576, 512, 64), (640, 576, 64), (704, 576, 64),
                  (768, 640, 128), (896, 768, 512)]
        for (dc, sc, n) in pieces:
            self.dma(win.ap[:, :, dc:dc + n], w[:, sc:sc + n].rearrange("(kc p) f -> p kc f", p=128),
                     writes=[win.buf], sembuf=win.buf, queue="pool")

    def phase_proj(self, b):
        self.P.tag = "proj"
        A = self.A
        mark = A.lo
        ropeb = [A.low("rope%d" % i, F32, [2, 512]) for i in range(1)]
        o = self.carry0
        A.hi = o
        self.q16 = A.tile("q16", o, BF16, [4, SEQ])
        self.k16 = A.tile("k16", o + 16384, BF16, [4, LTOT])
        self.vaug = A.tile("vaug", o + 16384 + self.KB16, BF16, [NKT, 2, 128])
        self.f16 = A.tile("f16", o + 16384 + self.KB16 + 8704, BF16, [NKT, 512])
        zg32s = [A.low("zg32_%d" % i, F32, [512]) for i in range(2)]
        zsq16s = [A.low("zsq16_%d" % i, BF16, [512]) for i in range(2)]
        zg16s = [A.low("zg16_%d" % i, BF16, [512]) for i in range(2)]
        rstds = [A.low("prstd%d" % i, F32, [512]) for i in range(2)]
        t1s = [A.low("t1_%d" % i, F32, [512]) for i in range(2)]
        t2s = [A.low("t2_%d" % i, F32, [512]) for i in range(2)]
        mine = ropeb + zg32s + zsq16s + zg16s + rstds + t1s + t2s
        d_rope3 = self.d_rope.rearrange("p (a b) -> p a b", a=2)
        win = self.win16
        vaug = self.vaug
        self.memset("pool", vaug.ap[:, 0:16, :, 64:128], 1.0, [vaug.buf])
        self.memset("pool", vaug.ap[:, 16, :, :], 0.0, [vaug.buf])
        self.memset("pool", vaug.ap[0:NMETA, 16, :, 64:128], 1.0, [vaug.buf])
        self.memset("pool", self.f16.ap[:, 16, :], 0.0, [self.f16.buf])
        self.memset("pool", self.k16.ap, 0.0, [self.k16.buf])
        zc = 0
        if b > 0:
            self.cp("pool", self.k16.ap[:, :, SEQ:SEQ + NMETA], self.kmeta.ap, [self.kmeta.buf], [self.k16.buf])
            self.cp("pool", vaug.ap[:, 16, :, :], self.vmeta.ap, [self.vmeta.buf], [vaug.buf])
            self.cp("pool", self.f16.ap[:, 16, :], self.fmeta.ap, [self.fmeta.buf], [self.f16.buf])
        for blk in range(NBLK if b == 0 else 4):
            N = blkN(blk)
            h16 = self.h16[blk]
            col0 = blk * 512 if blk < 4 else SEQ
            rope = ropeb[0]
            self.dma(rope.ap[:, :, 0:N], d_rope3[:, :, col0:col0 + N], writes=[rope.buf], sembuf=rope.buf)
            chunks = [("q", i) for i in range(4)] if blk < 4 else []
            chunks += [("k", 0), ("k", 1)]
            for kind, i in chunks:
                wc0 = i * 128 if kind == "q" else 512 + i * 128
                gcol = PC_QG if kind == "q" else PC_KG
                zp = self.ps[zc % 2]
                zg32, zsq16, zg16, rstd, t1, t2 = (zg32s[zc % 2], zsq16s[zc % 2], zg16s[zc % 2],
                                                   rstds[zc % 2], t1s[zc % 2], t2s[zc % 2])
                msp, zrp = self.ps[2 + 2 * (zc % 2)], self.ps[3 + 2 * (zc % 2)]
                zc += 1
                for kc in range(NCH):
                    self.mm(zp.ap[:, :N], win.ap[:, kc, wc0:wc0 + 128], h16.ap[:, kc, :N], kc == 0, kc == NCH - 1,
                            [win.buf, h16.cb[kc]], [zp.buf])
                self.act(zg32.ap[:, :N], zp.ap[:, :N], AF.Copy, [zp.buf, self.params.buf], [zg32.buf],
                         scale=self.params.ap[:, gcol:gcol + 1])
                self.act(zsq16.ap[:, :N], zp.ap[:, :N], AF.Square, [zp.buf], [zsq16.buf])
                self.act(zg16.ap[:, :N], zp.ap[:, :N], AF.Copy, [zp.buf, self.params.buf], [zg16.buf],
                         scale=self.params.ap[:, gcol:gcol + 1])
                self.mm(msp.ap[:, :N], self.cmat(CM_BD64), zsq16.ap[:, :N], True, True, [self.cm.buf, zsq16.buf], [msp.buf])
                self.mm(zrp.ap[:, :N], self.cmat(CM_ROT), zg16.ap[:, :N], True, True, [self.cm.buf, zg16.buf], [zrp.buf])
                self.act(rstd.ap[:, :N], msp.ap[:, :N], AF.Ln, [msp.buf, self.epsT.buf], [rstd.buf], bias=self.epsT.ap[:, 2:3])
                self.act(rstd.ap[:, :N], rstd.ap[:, :N], AF.Exp, [rstd.buf], [rstd.buf], scale=-0.5)
                self.tt("dve", t1.ap[:, :N], zg32.ap[:, :N], rope.ap[:, 0, 0:N], ALU.mult, [zg32.buf, rope.buf], [t1.buf])
                self.tt("dve", t2.ap[:, :N], zrp.ap[:, :N], rope.ap[:, 1, 0:N], ALU.mult, [zrp.buf, rope.buf], [t2.buf])
                self.tt("dve", t1.ap[:, :N], t1.ap[:, :N], t2.ap[:, :N], ALU.add, [t1.buf, t2.buf], [t1.buf])
                if kind == "q":
                    self.tt("dve", self.q16.ap[:, i, col0:col0 + N], t1.ap[:, :N], rstd.ap[:, :N], ALU.mult,
                            [t1.buf, rstd.buf], [self.q16.buf])
                else:
                    for e in range(2):
                        pr = slice(64 * e, 64 * e + 64)
                        self.tt("dve", self.k16.ap[pr, 2 * i + e, col0:col0 + N], t1.ap[pr, :N], rstd.ap[pr, :N], ALU.mult,
                                [t1.buf, rstd.buf], [self.k16.buf])
            ntile = 4 if blk < 4 else 1
            R = 128 if blk < 4 else NMETA
            vp = self.ps[6]
            for j in range(ntile):
                for kc in range(NCH):
                    self.mm(vp.ap[0:R, j * 128:(j + 1) * 128], h16.ap[:, kc, j * 128: j * 128 + R], win.ap[:, kc, 768:896],
                            kc == 0, kc == NCH - 1, [h16.cb[kc], win.buf], [vp.buf])
            kt0 = blk * 4 if blk < 4 else 16
            self.act(vaug.ap[0:R, kt0:kt0 + ntile, :, 0:64],
                     vp.ap[0:R, 0:ntile * 128].rearrange("p (t g d) -> p t g d", t=ntile, g=2),
                     AF.Copy, [vp.buf], [vaug.buf])
            for j in range(ntile):
                fp = self.ps[7]
                for kc in range(NCH):
                    self.mm(fp.ap[0:R, :], h16.ap[:, kc, j * 128: j * 128 + R], win.ap[:, kc, 896:1408],
                            kc == 0, kc == NCH - 1, [h16.cb[kc], win.buf], [fp.buf])
                if j % 2 == 0:
                    self.cp("dve", self.f16.ap[0:R, kt0 + j, :], fp.ap[0:R, :], [fp.buf], [self.f16.buf])
                else:
                    self.act(self.f16.ap[0:R, kt0 + j, :], fp.ap[0:R, :], AF.Copy, [fp.buf], [self.f16.buf])
        if b == 0 and self.npass > 1:
            self.cp("pool", self.kmeta.ap, self.k16.ap[:, :, SEQ:SEQ + NMETA], [self.k16.buf], [self.kmeta.buf])
            self.cp("pool", self.vmeta.ap, vaug.ap[:, 16, :, :], [vaug.buf], [self.vmeta.buf])
            self.cp("pool", self.fmeta.ap, self.f16.ap[:, 16, :], [self.f16.buf], [self.fmeta.buf])
        A.free(*mine)
        A.lo = mark
        A.free(self.win16)

    def phase_mix(self, b):
        self.P.tag = "mix"
        A = self.A
        mark = A.lo
        h16_off = [t.start for t in self.h16]
        r16 = A.region(self.h16[0].start, self.h16[-1].end)
        A.free(*self.h16)
        hole = A.region(self.win16.start, self.win16.end)
        wout16 = hole.low("wout16", BF16, [NCH, D])
        NP16 = 6
        p16 = [hole.low("p16_%d" % i, BF16, [512]) for i in range(NP16)]
        dC = r16.low("dftC", BF16, [NKT, 256])
        dS = r16.low("dftS", BF16, [NKT, 256])
        merged = [r16.low("merged16_%d" % c, BF16, [512]) for c in range(NCH)]
        ab16 = [r16.low("ab16_%d" % i, BF16, [512]) for i in range(2)]
        wstage = A.low("wstage", F32, [D])
        wo = self.d_w["w_out"]
        for c in range(NCH):
            self.dma(wstage.ap, wo[c * 128:(c + 1) * 128, :], writes=[wstage.buf], sembuf=wstage.buf)
            self.act(wout16.ap[:, c, :], wstage.ap, AF.Copy, [wstage.buf, self.params.buf], [wout16.buf],
                     scale=self.params.ap[:, PC_OG + c:PC_OG + c + 1])
        A.free(wstage)
        A.lo = mark
        rstdA = A.low("rstdA", F32, [512])
        rstdF = A.low("rstdF", F32, [512])
        u1 = A.low("u1", F32, [512])
        u2 = A.low("u2", F32, [512])
        rs = [u1, u2]
        lt = {}
        lt["sq16"] = [A.low("ln_sq16_%d" % c, BF16, [512]) for c in range(NCH)]
        lt["sq16c"] = lambda c: (lt["sq16"][c].ap, lt["sq16"][c].buf)
        lt["mean"] = [A.low("ln_mean", F32, [512])]
        lt["var"] = [A.low("ln_var", F32, [512])]
        lt["rstd"] = [A.low("ln_rstd", F32, [512])]
        lt["t"] = [u1, u2]
        sq16 = lt["sq16"]
        mine = [rstdA, rstdF, u1, u2] + lt["sq16"] + lt["mean"] + lt["var"] + lt["rstd"]
        q16, k16, vaug, f16 = self.q16, self.k16, self.vaug, self.f16
        sc = 0
        pc = 0
        for qb in range(4):
            s32 = self.h32[qb]
            self.P.tag = "mix.fourier"
            for half in range(2):
                idx = qb * 2 + half
                self.dma(dC.ap, self.d_dftc[idx].rearrange("p (t n) -> p t n", t=NKT), writes=[dC.buf], sembuf=dC.buf)
                self.dma(dS.ap, self.d_dfts[idx].rearrange("p (t n) -> p t n", t=NKT), writes=[dS.buf], sembuf=dS.buf)
                for c in range(4):
                    pf = self.ps[7]
                    for t in range(NKT):
                        self.mm(pf.ap[:, 0:256], f16.ap[:, t, c * 128:(c + 1) * 128], dC.ap[:, t, :],
                                t == 0, t == NKT - 1, [f16.buf, dC.buf], [pf.buf])
                    for t in range(NKT):
                        self.mm(pf.ap[:, 256:512], f16.ap[:, t, c * 128:(c + 1) * 128], dS.ap[:, t, :],
                                t == 0, t == NKT - 1, [f16.buf, dS.buf], [pf.buf])
                    abt = ab16[c % 2]
                    self.cp("dve", abt.ap, pf.ap, [pf.buf], [abt.buf])
                    self.mm(pf.ap[:, 0:256], self.cmat(CM_BDC), abt.ap[:, 0:256], True, False, [self.cm.buf, abt.buf], [pf.buf])
                    self.mm(pf.ap[:, 0:256], self.cmat(CM_BDSN), abt.ap[:, 256:512], False, True, [self.cm.buf, abt.buf], [pf.buf])
                    hs = slice(half * 256, (half + 1) * 256)
                    mt, st_ = merged[4 + c], sq16[4 + c]
                    self.cp("dve", mt.ap[:, hs], pf.ap[:, 0:256], [pf.buf], [mt.buf])
                    self.act(st_.ap[:, hs], pf.ap[:, 0:256], AF.Square, [pf.buf], [st_.buf])
            self.P.tag = "mix.attn"
            for hp in range(4):
                g = hp // 2
                ob = [self.ps[3 + 2 * (hp % 2)], self.ps[4 + 2 * (hp % 2)]]
                for t in range(NKT):
                    R = 128 if t < 16 else NMETA
                    kc0 = t * 128 if t < 16 else SEQ
                    for e in range(2):
                        sp = self.ps[sc % 3]
                        sc += 1
                        self.mm(sp.ap[0:R, :], k16.ap[:, 2 * g + e, kc0:kc0 + R], q16.ap[:, hp, qb * 512:(qb + 1) * 512],
                                True, True, [k16.buf, q16.buf], [sp.buf])
                        pt = p16[pc % NP16]
                        pc += 1
                        self.act(pt.ap[0:R, :], sp.ap[0:R, :], AF.Exp, [sp.buf], [pt.buf], scale=0.125)
                        self.mm(ob[e].ap, vaug.ap[:, t, g, :], pt.ap, t == 0, t == NKT - 1,
                                [vaug.buf, pt.buf], [ob[e].buf])
                mt = merged[hp]
                for e in range(2):
                    self.recip(rs[e].ap[64:128, :], ob[e].ap[64:128, :], [ob[e].buf], [rs[e].buf])
                    self.tt("dve", mt.ap[64 * e:64 * e + 64, :], ob[e].ap[0:64, :], rs[e].ap[64:128, :], ALU.mult,
                            [ob[e].buf, rs[e].buf], [mt.buf])
                self.act(sq16[hp].ap, mt.ap, AF.Square, [mt.buf], [sq16[hp].buf])
            self.P.tag = "mix.onorm"
            for (c0, rr) in ((0, rstdA), (4, rstdF)):
                pp = self.ps[7]
                for c in range(4):
                    self.mm(pp.ap, self.cmat(CM_ONES512), sq16[c0 + c].ap, c == 0, c == 3, [self.cm.buf, sq16[c0 + c].buf], [pp.buf])
                self.act(rr.ap, pp.ap, AF.Ln, [pp.buf, self.epsT.buf], [rr.buf], bias=self.epsT.ap[:, 2:3])
                self.act(rr.ap, rr.ap, AF.Exp, [rr.buf], [rr.buf], scale=-0.5)
            self.P.tag = "mix.wout"
            for c in range(NCH):
                rr = rstdA if c < 4 else rstdF
                self.tt("dve", merged[c].ap, merged[c].ap, rr.ap, ALU.mult, [merged[c].buf, rr.buf], [merged[c].buf])
            for dc in range(NCH):
                oa = self.ps[(5, 6)[dc % 2]]
                for c in range(NCH):
                    self.mm(oa.ap, wout16.ap[:, c, dc * 128:(dc + 1) * 128], merged[c].ap, c == 0, c == NCH - 1,
                            [wout16.buf, merged[c].buf], [oa.buf])
                self.stt(s32.ap[:, dc, :], oa.ap, C_MIX, s32.ap[:, dc, :], ALU.mult, ALU.add, [oa.buf, s32.cb[dc]], [s32.cb[dc]])
            self.P.tag = "mix.ln2"
            self.layer_norm(qb, PC_LN2_G, PC_LN2_B, 1, lt, make_h16=False, bank_a=7, bank_b=5,
                            s16c=lambda c: (merged[c].ap, merged[c].buf))
        A.free(*mine)
        A.lo = mark
        hole.free_all()
        r16.free_all()
        A.free(self.q16, self.k16, self.vaug, self.f16)
        A.hi = A.nbytes
        self.h16 = [A.tile("h16_%d" % k, h16_off[k], BF16, [NCH, blkN(k)]).chunked(NCH) for k in range(NBLK)]
        self.h16_stale = True

    def run_pass(self, b):
        mark0 = self.A.lo
        ffw = self.alloc_ffn_weights() if self.stage >= 1 else None
        self.phase_input(b)
        if self.stage >= 1:
            if self.stage >= 2:
                self.alloc_win()
            self.phase_ffn(b, "ff1", PC_LN1_G, PC_LN1_B, do_output=(self.stage == 1),
                           prefetch=self.load_win if self.stage >= 2 else None, weights=ffw, mark=mark0)
        if self.stage >= 2:
            self.phase_proj(b)
            self.phase_mix(b)
        if self.stage >= 3:
            self.phase_ffn(b, "ff2", PC_LN3_G, PC_LN3_B, do_output=True)
        if self.stage in (0, 2):
            mark = self.A.lo
            ostage = [self.A.low("ost%d" % i, F32, [D]) for i in range(2)]
            for blk in range(4):
                self.output_block(b, blk, ostage)
            self.A.free(*ostage)
            self.A.lo = mark


_CONST_CACHE = {}


def _consts():
    if _CONST_CACHE:
        return _CONST_CACHE
    bf = ml_dtypes.bfloat16
    ident = np.eye(128, dtype=np.float32)
    cm = np.zeros((6, 128, 128), np.float64)
    cm[CM_ONES1024] = 1.0 / 1024
    cm[CM_BD64][0:64, 0:64] = 1.0 / 64
    cm[CM_BD64][64:128, 64:128] = 1.0 / 64
    cm[CM_ONES512] = 1.0 / 512
    for i in range(64):
        cm[CM_ROT][2 * i + 1, 2 * i] = -1.0
        cm[CM_ROT][2 * i, 2 * i + 1] = 1.0
    cc = np.arange(64)
    ang = 2 * np.pi * ((cc[:, None] * cc[None, :]) % 64) / 64.0
    c64 = np.cos(ang) / 8.0
    s64 = np.sin(ang) / 8.0
    for g in range(2):
        cm[CM_BDC][64 * g:64 * g + 64, 64 * g:64 * g + 64] = c64
        cm[CM_BDSN][64 * g:64 * g + 64, 64 * g:64 * g + 64] = -s64
    cmats = np.ascontiguousarray(cm.transpose(1, 0, 2).reshape(128, 6 * 128)).astype(bf)
    t = np.arange(SEQ)
    row = (t // 64).astype(np.float32)
    col = (t % 64).astype(np.float32)
    inv_freq = (np.float32(10000.0) ** (-np.arange(16, dtype=np.float32) / np.float32(16))).astype(np.float32)
    angt = np.concatenate([row[:, None] * inv_freq, col[:, None] * inv_freq], axis=-1).astype(np.float32)
    angt = np.concatenate([angt, np.zeros((NMETA, 32), np.float32)], axis=0)
    pidx = (np.arange(128) % 64) // 2
    cosT = np.cos(angt)[:, pidx].T.astype(np.float32)
    sinT = np.sin(angt)[:, pidx].T.astype(np.float32)
    rope = np.ascontiguousarray(np.concatenate([cosT, sinT], axis=1))
    pos_rows = np.zeros((NKT, 128), np.int64)
    for kt in range(16):
        pos_rows[kt] = NMETA + kt * 128 + np.arange(128)
    pos_rows[16, :NMETA] = np.arange(NMETA)
    dftc = np.zeros((8, 128, NKT, 256), np.float32)
    dfts = np.zeros((8, 128, NKT, 256), np.float32)
    for idx in range(8):
        lp = NMETA + idx * 256 + np.arange(256)
        prod = (pos_rows[:, :, None] * lp[None, None, :]) % LTOT
        a = 2 * np.pi * prod.astype(np.float64) / LTOT
        cmat = (np.cos(a) / np.sqrt(LTOT)).astype(np.float32)
        smat = (np.sin(a) / np.sqrt(LTOT)).astype(np.float32)
        cmat[16, NMETA:, :] = 0
        smat[16, NMETA:, :] = 0
        dftc[idx] = cmat.transpose(1, 0, 2)
        dfts[idx] = smat.transpose(1, 0, 2)
    _CONST_CACHE.update(
        ident=ident, cmats=cmats, rope=rope,
        dftc=np.ascontiguousarray(dftc.reshape(8, 128, NKT * 256)).astype(bf),
        dfts=np.ascontiguousarray(dfts.reshape(8, 128, NKT * 256)).astype(bf))
    return _CONST_CACHE


def _fm(v):
    return np.ascontiguousarray(np.asarray(v, np.float32).reshape(-1, 128).T)


def _pack_params(inp):
    p = np.zeros((128, NPAR), np.float32)
    p[:, PC_EMB_G:PC_EMB_G + 8] = _fm(inp["ln_emb_g"])
    p[:, PC_EMB_B:PC_EMB_B + 8] = _fm(inp["ln_emb_b"])
    p[:, PC_LN1_G:PC_LN1_G + 8] = _fm(inp["ln1_g"][0])
    p[:, PC_LN1_B:PC_LN1_B + 8] = _fm(inp["ln1_b"][0])
    p[:, PC_LN2_G:PC_LN2_G + 8] = _fm(inp["ln2_g"][0])
    p[:, PC_LN2_B:PC_LN2_B + 8] = _fm(inp["ln2_b"][0])
    p[:, PC_LN3_G:PC_LN3_G + 8] = _fm(inp["ln3_g"][0])
    p[:, PC_LN3_B:PC_LN3_B + 8] = _fm(inp["ln3_b"][0])
    p[:, PC_QG] = np.tile(np.asarray(inp["q_norm_g"][0], np.float32), 2)
    p[:, PC_KG] = np.tile(np.asarray(inp["k_norm_g"][0], np.float32), 2)
    p[:, PC_OG:PC_OG + 4] = _fm(inp["attn_out_g"][0])
    p[:, PC_OG + 4:PC_OG + 8] = _fm(inp["fourier_out_g"][0])
    return p


_PROG_CACHE = {}


def _get_prog(npass, stage):
    key = (npass, stage)
    if key not in _PROG_CACHE:
        _PROG_CACHE[key] = Builder(npass, stage).build()
    return _PROG_CACHE[key]


def run(inputs, ncores=8, npass=2, stage=3):
    c = _consts()
    inp = {k: np.asarray(v) for k, v in inputs.items()}
    x = np.ascontiguousarray(inp["x"], dtype=np.float32)
    shared = dict(
        meta=np.ascontiguousarray(inp["meta_tokens"], dtype=np.float32),
        params=_pack_params(inp), ident=c["ident"], cmats=c["cmats"], rope=c["rope"],
        dftc=c["dftc"], dfts=c["dfts"],
        ff1_gate=np.ascontiguousarray(inp["ff1_gate"][0], dtype=np.float32),
        ff1_up=np.ascontiguousarray(inp["ff1_up"][0], dtype=np.float32),
        ff1_down=np.ascontiguousarray(inp["ff1_down"][0], dtype=np.float32),
        ff2_gate=np.ascontiguousarray(inp["ff2_gate"][0], dtype=np.float32),
        ff2_up=np.ascontiguousarray(inp["ff2_up"][0], dtype=np.float32),
        ff2_down=np.ascontiguousarray(inp["ff2_down"][0], dtype=np.float32),
        w_in=np.ascontiguousarray(inp["w_in"][0], dtype=np.float32),
        w_out=np.ascontiguousarray(inp["w_out"][0], dtype=np.float32),
    )
    nc = _get_prog(npass, stage)
    in_maps = []
    for i in range(ncores):
        m = dict(shared)
        m["x"] = np.ascontiguousarray(x[i * npass:(i + 1) * npass])
        in_maps.append(m)
    res = run_bass_kernel_spmd(nc, in_maps, core_ids=list(range(ncores)))
    return np.concatenate([np.asarray(r["out"]) for r in res.results], axis=0)


def kernel(**inputs):
    return run(inputs, ncores=8, npass=2, stage=3).astype(np.float32)
```

```python
import contextlib
import numpy as np
import ml_dtypes
import concourse.bass as bass
import concourse.mybir as mybir
from concourse.bass_utils import run_bass_kernel_spmd

F32 = mybir.dt.float32
BF16 = mybir.dt.bfloat16
AF = mybir.ActivationFunctionType
ALU = mybir.AluOpType

ENGS = ("sync", "act", "pe", "dve", "pool")


class SemSlot:
    __slots__ = ("dsem", "dcount")

    def __init__(self):
        self.dsem = None
        self.dcount = 0


class Buf:
    __slots__ = ("name", "w", "r", "slot", "excl")

    def __init__(self, name, excl=False, slot=None):
        self.name = name
        self.w = None
        self.r = []
        self.slot = slot if slot is not None else {}
        self.excl = excl


class Op:
    __slots__ = ("eng", "fn", "deps", "signal", "idx", "token", "is_dma", "seq", "cost", "succ", "nd", "ready", "fin",
                 "tag", "st", "crit")

    def __init__(self, eng, fn):
        self.eng = eng
        self.fn = fn
        self.deps = {}
        self.signal = False
        self.idx = 0
        self.token = None
        self.is_dma = False
        self.seq = 0
        self.cost = 100.0
        self.succ = []
        self.nd = 0
        self.ready = 0.0
        self.fin = 0.0
        self.tag = ""
        self.st = 0.0
        self.crit = None


STRICT_SAME_ENGINE = True
DMA_BW = 180.0
SEM_LAT = 225.0
PRIO_CP = True


class Prog:
    def __init__(self, nc):
        self.nc = nc
        self.ops = {e: [] for e in ENGS}
        self.all_dma_bufs = []
        self.nseq = 0
        self.tag = ""

    def _track(self, op, reads, writes):
        xr = [b for b in reads if b.excl and b not in writes]
        if xr:
            writes = list(writes) + xr
            reads = [b for b in reads if not b.excl]
        for b in reads:
            if b.w is not None:
                op.deps[b.w] = True
            b.r.append(op)
        for b in writes:
            if b.w is not None:
                op.deps.setdefault(b.w, False)
            for r in b.r:
                if r is not op:
                    op.deps.setdefault(r, False)
            b.w = op
            b.r = []

    def op(self, eng, fn, reads=(), writes=(), cost=100.0):
        o = Op(eng, fn)
        o.cost = cost
        o.tag = self.tag
        o.seq = self.nseq
        self.nseq += 1
        self._track(o, reads, writes)
        self.ops[eng].append(o)
        return o

    def dma(self, out, in_, reads=(), writes=(), sembuf=None, queue="sync", nbytes=0):
        o = Op(queue, None)
        o.is_dma = True
        o.tag = self.tag
        o.cost = float(nbytes)
        o.seq = self.nseq
        self.nseq += 1
        self._track(o, reads, writes)
        kind = "sw" if queue == "pool" else "hw"
        if kind not in sembuf.slot:
            sembuf.slot[kind] = SemSlot()
        slot = sembuf.slot[kind]
        if slot.dsem is None:
            self.all_dma_bufs.append(slot)
            slot.dsem = "pending"
        slot.dcount += 16
        o.token = (slot, slot.dcount)
        o.fn = (out, in_)
        self.ops[queue].append(o)
        return o

    @staticmethod
    def _skip(d, o, raw):
        if d.is_dma or d.eng != o.eng:
            return False
        if o.is_dma:
            return False
        if o.eng == "pe":
            return True
        return (not raw) and (not STRICT_SAME_ENGINE)

    def schedule(self):
        import heapq
        allops = [o for e in ENGS for o in self.ops[e]]
        for o in allops:
            o.succ = []
            o.ready = 0.0
        for o in allops:
            o.nd = len(o.deps)
            for d in o.deps:
                d.succ.append(o)
        byseq = sorted(allops, key=lambda o: o.seq)
        bl = {}
        for o in reversed(byseq):
            c = (o.cost / DMA_BW + 2000.0) if o.is_dma else o.cost
            m = 0.0
            for sc in o.succ:
                v = bl[sc]
                if v > m:
                    m = v
            bl[o] = c + m
        if PRIO_CP:
            for o in allops:
                o.seq = -bl[o] + o.seq * 1e-6
        later = {e: [] for e in ENGS}
        now = {e: [] for e in ENGS}
        free = {e: 0.0 for e in ENGS}
        for o in allops:
            if o.nd == 0:
                heapq.heappush(later[o.eng], (0.0, o.seq, o))
        new = {e: [] for e in ENGS}
        dma_free = 0.0
        left = len(allops)
        while left:
            best = None
            for e in ENGS:
                T = free[e]
                lt, nw = later[e], now[e]
                while lt and lt[0][0] <= T:
                    r, q, o = heapq.heappop(lt)
                    heapq.heappush(nw, (q, o))
                if nw:
                    st = T
                elif lt:
                    st = lt[0][0]
                else:
                    continue
                if best is None or st < best[0]:
                    best = (st, e)
            st, e = best
            if now[e]:
                q, o = heapq.heappop(now[e])
            else:
                r, q, o = heapq.heappop(later[e])
            if o.is_dma:
                free[e] = st + 80.0
                d0 = max(st + 80.0, dma_free)
                dma_free = d0 + o.cost / DMA_BW
                o.fin = dma_free + 2000.0
            else:
                o.fin = st + o.cost
                free[e] = o.fin
            new[e].append(o)
            o.st = st
            left -= 1
            for sc in o.succ:
                lat = 0.0 if (sc.eng == o.eng and not o.is_dma) else SEM_LAT
                if o.fin + lat > sc.ready:
                    sc.ready = o.fin + lat
                    sc.crit = o
                sc.nd -= 1
                if sc.nd == 0:
                    heapq.heappush(later[sc.eng], (sc.ready, sc.seq, sc))
        self.ops = new
        self.makespan = max(free.values())

    def finalize(self, final_waits=(), do_schedule=True):
        nc = self.nc
        if do_schedule:
            self.schedule()
        pos = {}
        for e in ENGS:
            for i, o in enumerate(self.ops[e]):
                pos[o] = i + 1
        wl = {}
        for e in ENGS:
            wpos = {}
            wdma = {}
            for o in self.ops[e]:
                lst = []
                for d, raw in o.deps.items():
                    if d.is_dma:
                        slot, val = d.token
                        if wdma.get(id(slot), 0) >= val:
                            continue
                        wdma[id(slot)] = val
                        lst.append(d)
                    else:
                        if self._skip(d, o, raw):
                            continue
                        if wpos.get(d.eng, 0) >= pos[d]:
                            continue
                        lst.append(d)
                best = {}
                out = []
                for d in lst:
                    if d.is_dma:
                        out.append(d)
                    elif d.eng not in best or pos[d] > pos[best[d.eng]]:
                        best[d.eng] = d
                for f, d in best.items():
                    wpos[f] = pos[d]
                    d.signal = True
                    out.append(d)
                wl[o] = out
        for e in ENGS:
            n = 0
            for o in self.ops[e]:
                if o.signal and not o.is_dma:
                    n += 1
                    o.idx = n
        with contextlib.ExitStack() as st:
            esem = {e: st.enter_context(nc.semaphore("es_" + e)) for e in ENGS}
            for i, b in enumerate(self.all_dma_bufs):
                b.dsem = st.enter_context(nc.semaphore("ds%d" % i))
            block = st.enter_context(nc.Block())

            def emit(engname, eng):
                for o in self.ops[engname]:
                    for d in wl[o]:
                        if d.is_dma:
                            eng.wait_ge(d.token[0].dsem, d.token[1])
                        else:
                            eng.wait_ge(esem[d.eng], d.idx)
                    if o.is_dma:
                        out, in_ = o.fn
                        eng.dma_start(out=out, in_=in_).then_inc(o.token[0].dsem, 16)
                    else:
                        ins = o.fn(eng)
                        if o.signal:
                            ins.then_inc(esem[engname], 1)
                if engname == "sync":
                    for d in final_waits:
                        eng.wait_ge(d.token[0].dsem, d.token[1])

            @block.sync
            def _(eng):
                emit("sync", eng)

            @block.scalar
            def _(eng):
                emit("act", eng)

            @block.tensor
            def _(eng):
                emit("pe", eng)

            @block.vector
            def _(eng):
                emit("dve", eng)

            @block.gpsimd
            def _(eng):
                emit("pool", eng)


class Tile:
    __slots__ = ("ap", "buf", "start", "end", "cb")

    def __init__(self, ap, buf, start, end):
        self.ap = ap
        self.buf = buf
        self.start = start
        self.end = end
        self.cb = None

    def chunked(self, n):
        if self.buf.name.startswith("h16"):
            self.cb = [self.buf] * n
            return self
        self.cb = [Buf(self.buf.name + "_c%d" % c, slot=self.buf.slot) for c in range(n)]
        for b in self.cb:
            b.r = list(self.buf.r)
        return self


_DTSIZE = {F32: 4, BF16: 2}


class Arena:
    def __init__(self, tensor, nbytes):
        self.t = tensor
        self.nbytes = nbytes
        self.live = []
        self.dead = []
        self.slots = {}
        self.lo = 0
        self.hi = nbytes
        self.peak = 0

    def tile(self, name, off, dtype, shape):
        n = 1
        for s in shape:
            n *= s
        nb = n * _DTSIZE[dtype]
        assert off % 4 == 0 and nb % 4 == 0, (name, off, nb)
        end = off + nb
        assert end <= self.nbytes, (name, end, self.nbytes)
        for t in self.live:
            assert end <= t.start or off >= t.end, ("overlap", name, t.buf.name)
        if off not in self.slots:
            self.slots[off] = {}
        b = Buf(name, slot=self.slots[off])
        keep = []
        for (s, e, ops) in self.dead:
            if not (end <= s or off >= e):
                for o in ops:
                    if o not in b.r:
                        b.r.append(o)
            keep.append((s, e, ops))
        ap = self.t[:, off // 4: end // 4]
        if dtype != F32:
            ap = ap.bitcast(dtype)
        if len(shape) == 2:
            ap = ap.rearrange("p (a b) -> p a b", a=shape[0])
        elif len(shape) == 3:
            ap = ap.rearrange("p (a b c) -> p a b c", a=shape[0], b=shape[1])
        tl = Tile(ap, b, off, end)
        self.live.append(tl)
        return tl

    def low(self, name, dtype, shape):
        t = self.tile(name, self.lo, dtype, shape)
        self.lo = (t.end + 63) // 64 * 64
        assert self.lo <= self.hi, ("arena full", name, self.lo, self.hi)
        self.peak = max(self.peak, self.lo + (self.nbytes - self.hi))
        return t

    def high(self, name, dtype, shape):
        n = 1
        for s in shape:
            n *= s
        nb = (n * _DTSIZE[dtype] + 63) // 64 * 64
        off = self.hi - nb
        assert off >= self.lo, ("arena full", name, self.lo, off)
        t = self.tile(name, off, dtype, shape)
        self.hi = off
        self.peak = max(self.peak, self.lo + (self.nbytes - self.hi))
        return t

    def region(self, start, end):
        return Region(self, start, end)

    def free(self, *tiles):
        for tl in tiles:
            self.live.remove(tl)
            ops = list(tl.buf.r)
            if tl.buf.w is not None:
                ops.append(tl.buf.w)
            for b in (tl.cb or ()):
                ops.extend(b.r)
                if b.w is not None:
                    ops.append(b.w)
            self.dead.append((tl.start, tl.end, ops))


class Region:
    def __init__(self, arena, start, end):
        self.A = arena
        self.lo = start
        self.end = end
        self.tiles = []

    def low(self, name, dtype, shape):
        t = self.A.tile(name, self.lo, dtype, shape)
        self.lo = (t.end + 63) // 64 * 64
        assert self.lo <= self.end, ("region full", name, self.lo, self.end)
        self.tiles.append(t)
        return t

    def free_all(self):
        self.A.free(*self.tiles)
        self.tiles = []


D = 1024
NCH = 8
SEQ = 2048
NMETA = 16
LTOT = SEQ + NMETA
DFF = 2816
NFC = DFF // 128
GS = 4
GROUPS = [(0, 2), (2, 4), (6, 4), (10, 4), (14, 4), (18, 4)]
NGRP = len(GROUPS)
INW = 1280
ALPHA = 2.0 ** 0.25
C_FFN = 0.5 / ALPHA
C_MIX = 1.0 / ALPHA
EPS_LN = 1e-5
EPS_LNS = 1e-5 / (ALPHA * ALPHA)
EPS_RMS = 1e-6
NBLK = 5
NKT = 17
ARENA_BYTES = 207 * 1024
NPAR = 80
PC_EMB_G, PC_EMB_B, PC_LN1_G, PC_LN1_B, PC_LN2_G, PC_LN2_B, PC_LN3_G, PC_LN3_B = 0, 8, 16, 24, 32, 40, 48, 56
PC_QG, PC_KG, PC_OG = 64, 65, 66
CM_ONES1024, CM_BD64, CM_ONES512, CM_ROT, CM_BDC, CM_BDSN = 0, 1, 2, 3, 4, 5


def blkN(blk):
    return 512 if blk < 4 else NMETA


class Builder:
    def __init__(self, npass, stage):
        self.npass = npass
        self.stage = stage
        self.nc = bass.Bass("TRN2", target_bir_lowering=False)
        nc = self.nc

        def din(name, shape, dt):
            return nc.dram_tensor(name, shape, dt, kind="ExternalInput").ap()

        self.d_x = din("x", [npass, SEQ, D], F32)
        self.d_meta = din("meta", [NMETA, D], F32)
        self.d_params = din("params", [128, NPAR], F32)
        self.d_ident = din("ident", [128, 128], F32)
        self.d_cmats = din("cmats", [128, 6 * 128], BF16)
        self.d_rope = din("rope", [128, 2 * LTOT], F32)
        self.d_dftc = din("dftc", [8, 128, NKT * 256], BF16)
        self.d_dfts = din("dfts", [8, 128, NKT * 256], BF16)
        self.d_w = {}
        for nm, shp in (("ff1_gate", [D, DFF]), ("ff1_up", [D, DFF]), ("ff1_down", [DFF, D]),
                        ("ff2_gate", [D, DFF]), ("ff2_up", [D, DFF]), ("ff2_down", [DFF, D]),
                        ("w_in", [D, INW]), ("w_out", [D, D])):
            self.d_w[nm] = din(nm, shp, F32)
        self.d_out = nc.dram_tensor("out", [npass, SEQ, D], F32, kind="ExternalOutput").ap()

    @staticmethod
    def _n(ap):
        n = 1
        for v in ap.shape[1:]:
            n *= v
        return n

    def mm(self, out, lhsT, rhs, start, stop, reads, writes):
        cost = max(self._n(rhs), 64) / 2.4 + 4.0
        self.P.op("pe", lambda e: e.matmul(out, lhsT=lhsT, rhs=rhs, start=start, stop=stop), reads, writes, cost)

    def tr(self, out, in_, ident, reads, writes):
        self.P.op("pe", lambda e: e.transpose(out, in_, ident), reads, writes, 180.0)

    def act(self, out, in_, func, reads, writes, scale=1.0, bias=None):
        cost = (self._n(out) + 100) / 1.2 + (0.0 if isinstance(scale, float) else 93.0)
        if bias is None:
            self.P.op("act", lambda e: e.activation(out=out, in_=in_, func=func, scale=scale), reads, writes, cost)
        else:
            self.P.op("act", lambda e: e.activation(out=out, in_=in_, func=func, scale=scale, bias=bias), reads, writes,
                      cost + 93.0)

    def tt(self, eng, out, in0, in1, op, reads, writes):
        n = self._n(out)
        cost = n / 0.96 + 160.0 if eng == "dve" else n * 2.4 + 100.0
        self.P.op(eng, lambda e: e.tensor_tensor(out=out, in0=in0, in1=in1, op=op), reads, writes, cost)

    def stt(self, out, in0, scalar, in1, op0, op1, reads, writes):
        self.P.op("dve", lambda e: e.scalar_tensor_tensor(out=out, in0=in0, scalar=scalar, in1=in1, op0=op0, op1=op1),
                  reads, writes, self._n(out) / 0.96 + 160.0)

    def cp(self, eng, out, in_, reads, writes):
        n = self._n(out)
        cost = n / 0.96 + 160.0 if eng == "dve" else n * 3.6 + 100.0
        self.P.op(eng, lambda e: e.tensor_copy(out=out, in_=in_), reads, writes, cost)

    def memset(self, eng, ap, val, writes):
        self.P.op(eng, lambda e: e.memset(ap, val), (), writes, 200.0)

    def recip(self, out, in_, reads, writes):
        self.P.op("dve", lambda e: e.reciprocal(out=out, in_=in_), reads, writes, self._n(out) * 6.5 + 60.0)

    def dma(self, out, in_, reads=(), writes=(), sembuf=None, queue="sync"):
        n = in_.shape[0] * self._n(in_)
        nbytes = n * (4 if in_.dtype == F32 else 2)
        return self.P.dma(out, in_, reads=reads, writes=writes, sembuf=sembuf, queue=queue, nbytes=nbytes)

    def build(self):
        nc = self.nc
        with contextlib.ExitStack() as st:
            arena_t = st.enter_context(nc.sbuf_tensor("arena", [128, ARENA_BYTES // 4], F32))
            pst = [st.enter_context(nc.psum_tensor("ps%d" % i, [128, 512], F32)) for i in range(8)]
            self.ps = [Tile(pst[i][:, :], Buf("ps%d" % i, excl=True), 0, 0) for i in range(8)]
            self.A = Arena(arena_t, ARENA_BYTES)
            self.P = Prog(nc)
            self.out_dmas = []
            self.setup_constants()
            for b in range(self.npass):
                self.run_pass(b)
            self.P.finalize(final_waits=self.out_dmas)
        return nc

    def setup_constants(self):
        A, P = self.A, self.P
        self.params = A.low("params", F32, [NPAR])
        self.ident = A.low("ident", F32, [128])
        self.cm = A.low("cmats", BF16, [6, 128])
        self.epsT = A.low("eps", F32, [4])
        self.dma(self.params.ap, self.d_params, writes=[self.params.buf], sembuf=self.params.buf)
        self.dma(self.ident.ap, self.d_ident, writes=[self.ident.buf], sembuf=self.ident.buf)
        self.dma(self.cm.ap, self.d_cmats.rearrange("p (a b) -> p a b", a=6), writes=[self.cm.buf], sembuf=self.cm.buf)
        self.memset("dve", self.epsT.ap[:, 0:1], EPS_LN, [self.epsT.buf])
        self.memset("dve", self.epsT.ap[:, 1:2], EPS_LNS, [self.epsT.buf])
        self.memset("dve", self.epsT.ap[:, 2:3], EPS_RMS, [self.epsT.buf])
        self.h32 = [A.low("h32_%d" % k, F32, [NCH, blkN(k)]).chunked(NCH) for k in range(NBLK)]
        self.h16 = [A.low("h16_%d" % k, BF16, [NCH, blkN(k)]).chunked(NCH) for k in range(NBLK)]
        self.kmeta = A.low("kmeta", BF16, [4, NMETA])
        self.vmeta = A.low("vmeta", BF16, [2, 128])
        self.fmeta = A.low("fmeta", BF16, [512])
        self.base_lo = A.lo

    def cmat(self, k):
        return self.cm.ap[:, k, :]

    def ln_temps(self, nt=3):
        A = self.A
        t = {}
        t["sq16"] = [A.low("ln_sq16_%d" % c, BF16, [512]) for c in range(NCH)]
        sqt = t["sq16"]
        t["sq16c"] = lambda c: (sqt[c].ap, sqt[c].buf)
        t["mean"] = [A.low("ln_mean%d" % i, F32, [512]) for i in range(2)]
        t["var"] = [A.low("ln_var%d" % i, F32, [512]) for i in range(2)]
        t["rstd"] = [A.low("ln_rstd%d" % i, F32, [512]) for i in range(2)]
        t["t"] = [A.low("ln_t%d" % i, F32, [512]) for i in range(nt)]
        t["all"] = t["sq16"] + t["mean"] + t["var"] + t["rstd"] + t["t"]
        return t

    def layer_norm(self, blk, gcol, bcol, epscol, lt, make_h16=True, bank_a=6, bank_b=7, s16c=None):
        N = blkN(blk)
        s32 = self.h32[blk]
        if s16c is None:
            h16t = self.h16[blk]
            s16c = lambda c: (h16t.ap[:, c, :N], h16t.cb[c])
        sq16c = lt["sq16c"]
        pa, pb = self.ps[bank_a], self.ps[bank_b]
        ones = self.cmat(CM_ONES1024)
        for c in range(NCH):
            a16_, b16_ = s16c(c)
            self.cp("dve", a16_[:, :N] if N < 512 else a16_, s32.ap[:, c, :N], [s32.cb[c]], [b16_])
            q16_, qb_ = sq16c(c)
            self.act(q16_[:, :N] if N < 512 else q16_, s32.ap[:, c, :N], AF.Square, [s32.cb[c]], [qb_])
        for c in range(NCH):
            a16_, b16_ = s16c(c)
            self.mm(pa.ap[:, :N], ones, a16_[:, :N] if N < 512 else a16_, c == 0, c == NCH - 1, [self.cm.buf, b16_], [pa.buf])
        for c in range(NCH):
            q16_, qb_ = sq16c(c)
            self.mm(pb.ap[:, :N], ones, q16_[:, :N] if N < 512 else q16_, c == 0, c == NCH - 1, [self.cm.buf, qb_], [pb.buf])
        self.ln_count = getattr(self, "ln_count", 0) + 1
        k = self.ln_count % len(lt["mean"])
        mean, var, rstd = lt["mean"][k], lt["var"][k], lt["rstd"][k]
        self.act(mean.ap[:, :N], pa.ap[:, :N], AF.Copy, [pa.buf], [mean.buf])
        self.tt("dve", var.ap[:, :N], mean.ap[:, :N], mean.ap[:, :N], ALU.mult, [mean.buf], [var.buf])
        self.tt("dve", var.ap[:, :N], pb.ap[:, :N], var.ap[:, :N], ALU.subtract, [pb.buf, var.buf], [var.buf])
        self.act(var.ap[:, :N], var.ap[:, :N], AF.Ln, [var.buf, self.epsT.buf], [var.buf],
                 bias=self.epsT.ap[:, epscol:epscol + 1])
        self.act(rstd.ap[:, :N], var.ap[:, :N], AF.Exp, [var.buf], [rstd.buf], scale=-0.5)
        nt = len(lt["t"])
        for c in range(NCH):
            t = lt["t"][c % nt]
            self.tt("dve", t.ap[:, :N], s32.ap[:, c, :N], mean.ap[:, :N], ALU.subtract, [s32.cb[c], mean.buf], [t.buf])
            self.tt("dve", t.ap[:, :N], t.ap[:, :N], rstd.ap[:, :N], ALU.mult, [t.buf, rstd.buf], [t.buf])
            self.act(s32.ap[:, c, :N], t.ap[:, :N], AF.Identity, [t.buf, self.params.buf], [s32.cb[c]],
                     scale=self.params.ap[:, gcol + c:gcol + c + 1], bias=self.params.ap[:, bcol + c:bcol + c + 1])
            if make_h16:
                self.act(self.h16[blk].ap[:, c, :N], t.ap[:, :N], AF.Identity, [t.buf, self.params.buf],
                         [self.h16[blk].cb[c]],
                         scale=self.params.ap[:, gcol + c:gcol + c + 1], bias=self.params.ap[:, bcol + c:bcol + c + 1])

    def phase_input(self, b):
        self.P.tag = "input"
        A = self.A
        mark = A.lo
        NX = 5
        xin = [A.low("xin%d" % i, F32, [D]) for i in range(NX)]
        lt = self.ln_temps()
        mine = xin + lt["all"]
        k = 0
        for blk in range(NBLK if b == 0 else 4):
            ntile = 4 if blk < 4 else 1
            for j in range(ntile):
                xt = xin[k % NX]
                k += 1
                R = 128 if blk < 4 else NMETA
                if blk < 4:
                    src = self.d_x[b, blk * 512 + j * 128: blk * 512 + (j + 1) * 128, :]
                else:
                    src = self.d_meta
                self.dma(xt.ap[0:R, :], src, writes=[xt.buf], sembuf=xt.buf)
                for half in range(2):
                    pt = self.ps[4 + half]
                    for cc in range(4):
                        c = half * 4 + cc
                        self.tr(pt.ap[:, cc * 128: cc * 128 + R], xt.ap[0:R, c * 128:(c + 1) * 128],
                                self.ident.ap[0:R, 0:R], [xt.buf, self.ident.buf], [pt.buf])
                    dst = self.h32[blk].ap[:, half * 4:(half + 1) * 4, j * 128: j * 128 + R]
                    srcp = pt.ap.rearrange("p (a b) -> p a b", a=4)[:, :, 0:R]
                    eng = "act" if half == 0 else "dve"
                    if eng == "act":
                        self.act(dst, srcp, AF.Copy, [pt.buf], self.h32[blk].cb[half * 4:(half + 1) * 4])
                    else:
                        self.cp("dve", dst, srcp, [pt.buf], self.h32[blk].cb[half * 4:(half + 1) * 4])
            self.layer_norm(blk, PC_EMB_G, PC_EMB_B, 0, lt)
        A.free(*mine)
        A.lo = mark

    def alloc_ffn_weights(self):
        A = self.A
        NS = 2
        wg16, wu16, wd16 = [], [], []
        for i in range(NS):
            wg16.append([A.low("wg16_%d_%d" % (i, j), BF16, [NCH, 128]) for j in range(GS)])
            wu16.append([A.low("wu16_%d_%d" % (i, j), BF16, [NCH, 128]) for j in range(GS)])
            wd16.append([A.low("wd16_%d_%d" % (i, j), BF16, [D]) for j in range(GS)])
        return wg16, wu16, wd16

    def phase_ffn(self, b, which, gcol, bcol, do_output, prefetch=None, weights=None, mark=None):
        self.P.tag = which
        A = self.A
        if mark is None:
            mark = A.lo
        wg_d, wu_d, wd_d = self.d_w[which + "_gate"], self.d_w[which + "_up"], self.d_w[which + "_down"]
        NS = 2
        wg16, wu16, wd16 = weights if weights is not None else self.alloc_ffn_weights()
        NA = 3 if do_output else 2
        a16 = [[A.low("a16_%d_%d" % (i, j), BF16, [512]) for j in range(GS)] for i in range(NA)]
        sg = [A.low("sg%d" % i, F32, [512]) for i in range(NA)]
        lt = self.ln_temps(nt=3 if do_output else 2)
        ostage = [A.low("ost%d" % i, F32, [D]) for i in range(2)] if do_output else []
        mine = [t for l3 in (wg16, wu16, wd16) for l2 in l3 for t in l2] + [t for st_ in a16 for t in st_] + \
            sg + ostage + lt["all"]

        def load_group(grp, slot):
            fc0, gs = GROUPS[grp]
            for j in range(gs):
                f0 = (fc0 + j) * 128
                self.dma(wg16[slot][j].ap, wg_d[:, f0:f0 + 128].rearrange("(kc p) f -> p kc f", p=128),
                         writes=[wg16[slot][j].buf], sembuf=wg16[slot][j].buf, queue="pool")
                self.dma(wu16[slot][j].ap, wu_d[:, f0:f0 + 128].rearrange("(kc p) f -> p kc f", p=128),
                         writes=[wu16[slot][j].buf], sembuf=wu16[slot][j].buf, queue="pool")
            for j in range(gs):
                f0 = (fc0 + j) * 128
                self.dma(wd16[slot][j].ap, wd_d[f0:f0 + 128, :],
                         writes=[wd16[slot][j].buf], sembuf=wd16[slot][j].buf, queue="pool")

        load_group(0, 0)
        nblk = 4 if (do_output or b > 0) else NBLK
        if getattr(self, "h16_stale", False):
            for blk in range(4):
                for c in range(NCH):
                    if c % 2:
                        self.act(self.h16[blk].ap[:, c, :], self.h32[blk].ap[:, c, :], AF.Copy,
                                 [self.h32[blk].cb[c]], [self.h16[blk].cb[c]])
                    else:
                        self.cp("dve", self.h16[blk].ap[:, c, :], self.h32[blk].ap[:, c, :],
                                [self.h32[blk].cb[c]], [self.h16[blk].cb[c]])
            self.h16_stale = False
        cnt = 0
        ycnt = 0
        gj = 0
        for grp in range(NGRP):
            slot = grp % NS
            gs = GROUPS[grp][1]
            if grp + 1 < NGRP:
                load_group(grp + 1, (grp + 1) % NS)
            if prefetch is not None and grp == NGRP - 2:
                prefetch()
            for blk in range(nblk):
                N = blkN(blk)
                h16 = self.h16[blk]
                s32 = self.h32[blk]
                aset = a16[cnt % NA]
                for j in range(gs):
                    gp = self.ps[0 + gj % 2]
                    up = self.ps[2 + gj % 2]
                    for kc in range(NCH):
                        self.mm(gp.ap[:, :N], wg16[slot][j].ap[:, kc, :], h16.ap[:, kc, :N],
                                kc == 0, kc == NCH - 1, [wg16[slot][j].buf, h16.cb[kc]], [gp.buf])
                    for kc in range(NCH):
                        self.mm(up.ap[:, :N], wu16[slot][j].ap[:, kc, :], h16.ap[:, kc, :N],
                                kc == 0, kc == NCH - 1, [wu16[slot][j].buf, h16.cb[kc]], [up.buf])
                    sgt = sg[gj % NA]
                    gj += 1
                    self.act(sgt.ap[:, :N], gp.ap[:, :N], AF.Silu, [gp.buf], [sgt.buf])
                    self.tt("dve", aset[j].ap[:, :N], sgt.ap[:, :N], up.ap[:, :N], ALU.mult,
                            [sgt.buf, up.buf], [aset[j].buf])
                for dc in range(NCH):
                    yp = self.ps[4 + ycnt % 2]
                    ycnt += 1
                    for j in range(gs):
                        self.mm(yp.ap[:, :N], wd16[slot][j].ap[:, dc * 128:(dc + 1) * 128], aset[j].ap[:, :N],
                                j == 0, j == gs - 1, [wd16[slot][j].buf, aset[j].buf], [yp.buf])
                    self.stt(s32.ap[:, dc, :N], yp.ap[:, :N], C_FFN, s32.ap[:, dc, :N], ALU.mult, ALU.add,
                             [yp.buf, s32.cb[dc]], [s32.cb[dc]])
                cnt += 1
                if grp == NGRP - 1:
                    self.P.tag = which + ".ln"
                    if blk == nblk - 1:
                        self.layer_norm(blk, gcol, bcol, 1, lt, make_h16=not do_output, bank_a=0, bank_b=2)
                    else:
                        self.layer_norm(blk, gcol, bcol, 1, lt, make_h16=not do_output)
                    if do_output:
                        self.P.tag = which + ".out"
                        self.output_block(b, blk, ostage)
                    self.P.tag = which
        A.free(*mine)
        A.lo = mark

    def output_block(self, b, blk, ostage):
        for j in range(4):
            ot = ostage[j % 2]
            for half in range(2):
                pt = self.ps[6 + half]
                for cc in range(4):
                    c = half * 4 + cc
                    self.tr(pt.ap[:, cc * 128:(cc + 1) * 128], self.h32[blk].ap[:, c, j * 128:(j + 1) * 128],
                            self.ident.ap, [self.h32[blk].cb[c], self.ident.buf], [pt.buf])
                if half == 0:
                    self.act(ot.ap[:, 0:512], pt.ap, AF.Copy, [pt.buf], [ot.buf])
                else:
                    self.cp("dve", ot.ap[:, 512:1024], pt.ap, [pt.buf], [ot.buf])
            d = self.dma(self.d_out[b, blk * 512 + j * 128: blk * 512 + (j + 1) * 128, :], ot.ap,
                           reads=[ot.buf], sembuf=ot.buf)
            self.out_dmas.append(d)

    KB16 = 4 * LTOT * 2
    CARRY = 16384 + KB16 + 8704 + 17408

    def alloc_win(self):
        A = self.A
        off = A.nbytes - NCH * 1408 * 2
        self.win16 = A.tile("win16", off, BF16, [NCH, 1408])
        self.carry0 = off - self.CARRY
        A.hi = off

    def load_win(self):
        w = self.d_w["w_in"]
        win = self.win16
        pieces = [(0, 0, 512), (512, 512, 64), (576, 512, 64), (640, 576, 64), (704, 576, 64),
                  (768, 640, 128), (896, 768, 512)]
        for (dc, sc, n) in pieces:
            self.dma(win.ap[:, :, dc:dc + n], w[:, sc:sc + n].rearrange("(kc p) f -> p kc f", p=128),
                     writes=[win.buf], sembuf=win.buf, queue="pool")

    def phase_proj(self, b):
        self.P.tag = "proj"
        A = self.A
        mark = A.lo
        ropeb = [A.low("rope%d" % i, F32, [2, 512]) for i in range(1)]
        o = self.carry0
        A.hi = o
        self.q16 = A.tile("q16", o, BF16, [4, SEQ])
        self.k16 = A.tile("k16", o + 16384, BF16, [4, LTOT])
        self.vaug = A.tile("vaug", o + 16384 + self.KB16, BF16, [NKT, 2, 128])
        self.f16 = A.tile("f16", o + 16384 + self.KB16 + 8704, BF16, [NKT, 512])
        zg32s = [A.low("zg32_%d" % i, F32, [512]) for i in range(2)]
        zsq16s = [A.low("zsq16_%d" % i, BF16, [512]) for i in range(2)]
        zg16s = [A.low("zg16_%d" % i, BF16, [512]) for i in range(2)]
        rstds = [A.low("prstd%d" % i, F32, [512]) for i in range(2)]
        t1s = [A.low("t1_%d" % i, F32, [512]) for i in range(2)]
        t2s = [A.low("t2_%d" % i, F32, [512]) for i in range(2)]
        mine = ropeb + zg32s + zsq16s + zg16s + rstds + t1s + t2s
        d_rope3 = self.d_rope.rearrange("p (a b) -> p a b", a=2)
        win = self.win16
        vaug = self.vaug
        self.memset("pool", vaug.ap[:, 0:16, :, 64:128], 1.0, [vaug.buf])
        self.memset("pool", vaug.ap[:, 16, :, :], 0.0, [vaug.buf])
        self.memset("pool", vaug.ap[0:NMETA, 16, :, 64:128], 1.0, [vaug.buf])
        self.memset("pool", self.f16.ap[:, 16, :], 0.0, [self.f16.buf])
        self.memset("pool", self.k16.ap, 0.0, [self.k16.buf])
        zc = 0
        if b > 0:
            self.cp("pool", self.k16.ap[:, :, SEQ:SEQ + NMETA], self.kmeta.ap, [self.kmeta.buf], [self.k16.buf])
            self.cp("pool", vaug.ap[:, 16, :, :], self.vmeta.ap, [self.vmeta.buf], [vaug.buf])
            self.cp("pool", self.f16.ap[:, 16, :], self.fmeta.ap, [self.fmeta.buf], [self.f16.buf])
        for blk in range(NBLK if b == 0 else 4):
            N = blkN(blk)
            h16 = self.h16[blk]
            col0 = blk * 512 if blk < 4 else SEQ
            rope = ropeb[0]
            self.dma(rope.ap[:, :, 0:N], d_rope3[:, :, col0:col0 + N], writes=[rope.buf], sembuf=rope.buf)
            chunks = [("q", i) for i in range(4)] if blk < 4 else []
            chunks += [("k", 0), ("k", 1)]
            for kind, i in chunks:
                wc0 = i * 128 if kind == "q" else 512 + i * 128
                gcol = PC_QG if kind == "q" else PC_KG
                zp = self.ps[zc % 2]
                zg32, zsq16, zg16, rstd, t1, t2 = (zg32s[zc % 2], zsq16s[zc % 2], zg16s[zc % 2],
                                                   rstds[zc % 2], t1s[zc % 2], t2s[zc % 2])
                msp, zrp = self.ps[2 + 2 * (zc % 2)], self.ps[3 + 2 * (zc % 2)]
                zc += 1
                for kc in range(NCH):
                    self.mm(zp.ap[:, :N], win.ap[:, kc, wc0:wc0 + 128], h16.ap[:, kc, :N], kc == 0, kc == NCH - 1,
                            [win.buf, h16.cb[kc]], [zp.buf])
                self.act(zg32.ap[:, :N], zp.ap[:, :N], AF.Copy, [zp.buf, self.params.buf], [zg32.buf],
                         scale=self.params.ap[:, gcol:gcol + 1])
                self.act(zsq16.ap[:, :N], zp.ap[:, :N], AF.Square, [zp.buf], [zsq16.buf])
                self.act(zg16.ap[:, :N], zp.ap[:, :N], AF.Copy, [zp.buf, self.params.buf], [zg16.buf],
                         scale=self.params.ap[:, gcol:gcol + 1])
                self.mm(msp.ap[:, :N], self.cmat(CM_BD64), zsq16.ap[:, :N], True, True, [self.cm.buf, zsq16.buf], [msp.buf])
                self.mm(zrp.ap[:, :N], self.cmat(CM_ROT), zg16.ap[:, :N], True, True, [self.cm.buf, zg16.buf], [zrp.buf])
                self.act(rstd.ap[:, :N], msp.ap[:, :N], AF.Ln, [msp.buf, self.epsT.buf], [rstd.buf], bias=self.epsT.ap[:, 2:3])
                self.act(rstd.ap[:, :N], rstd.ap[:, :N], AF.Exp, [rstd.buf], [rstd.buf], scale=-0.5)
                self.tt("dve", t1.ap[:, :N], zg32.ap[:, :N], rope.ap[:, 0, 0:N], ALU.mult, [zg32.buf, rope.buf], [t1.buf])
                self.tt("dve", t2.ap[:, :N], zrp.ap[:, :N], rope.ap[:, 1, 0:N], ALU.mult, [zrp.buf, rope.buf], [t2.buf])
                self.tt("dve", t1.ap[:, :N], t1.ap[:, :N], t2.ap[:, :N], ALU.add, [t1.buf, t2.buf], [t1.buf])
                if kind == "q":
                    self.tt("dve", self.q16.ap[:, i, col0:col0 + N], t1.ap[:, :N], rstd.ap[:, :N], ALU.mult,
                            [t1.buf, rstd.buf], [self.q16.buf])
                else:
                    for e in range(2):
                        pr = slice(64 * e, 64 * e + 64)
                        self.tt("dve", self.k16.ap[pr, 2 * i + e, col0:col0 + N], t1.ap[pr, :N], rstd.ap[pr, :N], ALU.mult,
                                [t1.buf, rstd.buf], [self.k16.buf])
            ntile = 4 if blk < 4 else 1
            R = 128 if blk < 4 else NMETA
            vp = self.ps[6]
            for j in range(ntile):
                for kc in range(NCH):
                    self.mm(vp.ap[0:R, j * 128:(j + 1) * 128], h16.ap[:, kc, j * 128: j * 128 + R], win.ap[:, kc, 768:896],
                            kc == 0, kc == NCH - 1, [h16.cb[kc], win.buf], [vp.buf])
            kt0 = blk * 4 if blk < 4 else 16
            self.act(vaug.ap[0:R, kt0:kt0 + ntile, :, 0:64],
                     vp.ap[0:R, 0:ntile * 128].rearrange("p (t g d) -> p t g d", t=ntile, g=2),
                     AF.Copy, [vp.buf], [vaug.buf])
            for j in range(ntile):
                fp = self.ps[7]
                for kc in range(NCH):
                    self.mm(fp.ap[0:R, :], h16.ap[:, kc, j * 128: j * 128 + R], win.ap[:, kc, 896:1408],
                            kc == 0, kc == NCH - 1, [h16.cb[kc], win.buf], [fp.buf])
                if j % 2 == 0:
                    self.cp("dve", self.f16.ap[0:R, kt0 + j, :], fp.ap[0:R, :], [fp.buf], [self.f16.buf])
                else:
                    self.act(self.f16.ap[0:R, kt0 + j, :], fp.ap[0:R, :], AF.Copy, [fp.buf], [self.f16.buf])
        if b == 0 and self.npass > 1:
            self.cp("pool", self.kmeta.ap, self.k16.ap[:, :, SEQ:SEQ + NMETA], [self.k16.buf], [self.kmeta.buf])
            self.cp("pool", self.vmeta.ap, vaug.ap[:, 16, :, :], [vaug.buf], [self.vmeta.buf])
            self.cp("pool", self.fmeta.ap, self.f16.ap[:, 16, :], [self.f16.buf], [self.fmeta.buf])
        A.free(*mine)
        A.lo = mark
        A.free(self.win16)

    def phase_mix(self, b):
        self.P.tag = "mix"
        A = self.A
        mark = A.lo
        h16_off = [t.start for t in self.h16]
        r16 = A.region(self.h16[0].start, self.h16[-1].end)
        A.free(*self.h16)
        hole = A.region(self.win16.start, self.win16.end)
        wout16 = hole.low("wout16", BF16, [NCH, D])
        NP16 = 6
        p16 = [hole.low("p16_%d" % i, BF16, [512]) for i in range(NP16)]
        dC = r16.low("dftC", BF16, [NKT, 256])
        dS = r16.low("dftS", BF16, [NKT, 256])
        merged = [r16.low("merged16_%d" % c, BF16, [512]) for c in range(NCH)]
        ab16 = [r16.low("ab16_%d" % i, BF16, [512]) for i in range(2)]
        wstage = A.low("wstage", F32, [D])
        wo = self.d_w["w_out"]
        for c in range(NCH):
            self.dma(wstage.ap, wo[c * 128:(c + 1) * 128, :], writes=[wstage.buf], sembuf=wstage.buf)
            self.act(wout16.ap[:, c, :], wstage.ap, AF.Copy, [wstage.buf, self.params.buf], [wout16.buf],
                     scale=self.params.ap[:, PC_OG + c:PC_OG + c + 1])
        A.free(wstage)
        A.lo = mark
        rstdA = A.low("rstdA", F32, [512])
        rstdF = A.low("rstdF", F32, [512])
        u1 = A.low("u1", F32, [512])
        u2 = A.low("u2", F32, [512])
        rs = [u1, u2]
        lt = {}
        lt["sq16"] = [A.low("ln_sq16_%d" % c, BF16, [512]) for c in range(NCH)]
        lt["sq16c"] = lambda c: (lt["sq16"][c].ap, lt["sq16"][c].buf)
        lt["mean"] = [A.low("ln_mean", F32, [512])]
        lt["var"] = [A.low("ln_var", F32, [512])]
        lt["rstd"] = [A.low("ln_rstd", F32, [512])]
        lt["t"] = [u1, u2]
        sq16 = lt["sq16"]
        mine = [rstdA, rstdF, u1, u2] + lt["sq16"] + lt["mean"] + lt["var"] + lt["rstd"]
        q16, k16, vaug, f16 = self.q16, self.k16, self.vaug, self.f16
        sc = 0
        pc = 0
        for qb in range(4):
            s32 = self.h32[qb]
            self.P.tag = "mix.fourier"
            for half in range(2):
                idx = qb * 2 + half
                self.dma(dC.ap, self.d_dftc[idx].rearrange("p (t n) -> p t n", t=NKT), writes=[dC.buf], sembuf=dC.buf)
                self.dma(dS.ap, self.d_dfts[idx].rearrange("p (t n) -> p t n", t=NKT), writes=[dS.buf], sembuf=dS.buf)
                for c in range(4):
                    pf = self.ps[7]
                    for t in range(NKT):
                        self.mm(pf.ap[:, 0:256], f16.ap[:, t, c * 128:(c + 1) * 128], dC.ap[:, t, :],
                                t == 0, t == NKT - 1, [f16.buf, dC.buf], [pf.buf])
                    for t in range(NKT):
                        self.mm(pf.ap[:, 256:512], f16.ap[:, t, c * 128:(c + 1) * 128], dS.ap[:, t, :],
                                t == 0, t == NKT - 1, [f16.buf, dS.buf], [pf.buf])
                    abt = ab16[c % 2]
                    self.cp("dve", abt.ap, pf.ap, [pf.buf], [abt.buf])
                    self.mm(pf.ap[:, 0:256], self.cmat(CM_BDC), abt.ap[:, 0:256], True, False, [self.cm.buf, abt.buf], [pf.buf])
                    self.mm(pf.ap[:, 0:256], self.cmat(CM_BDSN), abt.ap[:, 256:512], False, True, [self.cm.buf, abt.buf], [pf.buf])
                    hs = slice(half * 256, (half + 1) * 256)
                    mt, st_ = merged[4 + c], sq16[4 + c]
                    self.cp("dve", mt.ap[:, hs], pf.ap[:, 0:256], [pf.buf], [mt.buf])
                    self.act(st_.ap[:, hs], pf.ap[:, 0:256], AF.Square, [pf.buf], [st_.buf])
            self.P.tag = "mix.attn"
            for hp in range(4):
                g = hp // 2
                ob = [self.ps[3 + 2 * (hp % 2)], self.ps[4 + 2 * (hp % 2)]]
                for t in range(NKT):
                    R = 128 if t < 16 else NMETA
                    kc0 = t * 128 if t < 16 else SEQ
                    for e in range(2):
                        sp = self.ps[sc % 3]
                        sc += 1
                        self.mm(sp.ap[0:R, :], k16.ap[:, 2 * g + e, kc0:kc0 + R], q16.ap[:, hp, qb * 512:(qb + 1) * 512],
                                True, True, [k16.buf, q16.buf], [sp.buf])
                        pt = p16[pc % NP16]
                        pc += 1
                        self.act(pt.ap[0:R, :], sp.ap[0:R, :], AF.Exp, [sp.buf], [pt.buf], scale=0.125)
                        self.mm(ob[e].ap, vaug.ap[:, t, g, :], pt.ap, t == 0, t == NKT - 1,
                                [vaug.buf, pt.buf], [ob[e].buf])
                mt = merged[hp]
                for e in range(2):
                    self.recip(rs[e].ap[64:128, :], ob[e].ap[64:128, :], [ob[e].buf], [rs[e].buf])
                    self.tt("dve", mt.ap[64 * e:64 * e + 64, :], ob[e].ap[0:64, :], rs[e].ap[64:128, :], ALU.mult,
                            [ob[e].buf, rs[e].buf], [mt.buf])
                self.act(sq16[hp].ap, mt.ap, AF.Square, [mt.buf], [sq16[hp].buf])
            self.P.tag = "mix.onorm"
            for (c0, rr) in ((0, rstdA), (4, rstdF)):
                pp = self.ps[7]
                for c in range(4):
                    self.mm(pp.ap, self.cmat(CM_ONES512), sq16[c0 + c].ap, c == 0, c == 3, [self.cm.buf, sq16[c0 + c].buf], [pp.buf])
                self.act(rr.ap, pp.ap, AF.Ln, [pp.buf, self.epsT.buf], [rr.buf], bias=self.epsT.ap[:, 2:3])
                self.act(rr.ap, rr.ap, AF.Exp, [rr.buf], [rr.buf], scale=-0.5)
            self.P.tag = "mix.wout"
            for c in range(NCH):
                rr = rstdA if c < 4 else rstdF
                self.tt("dve", merged[c].ap, merged[c].ap, rr.ap, ALU.mult, [merged[c].buf, rr.buf], [merged[c].buf])
            for dc in range(NCH):
                oa = self.ps[(5, 6)[dc % 2]]
                for c in range(NCH):
                    self.mm(oa.ap, wout16.ap[:, c, dc * 128:(dc + 1) * 128], merged[c].ap, c == 0, c == NCH - 1,
                            [wout16.buf, merged[c].buf], [oa.buf])
                self.stt(s32.ap[:, dc, :], oa.ap, C_MIX, s32.ap[:, dc, :], ALU.mult, ALU.add, [oa.buf, s32.cb[dc]], [s32.cb[dc]])
            self.P.tag = "mix.ln2"
            self.layer_norm(qb, PC_LN2_G, PC_LN2_B, 1, lt, make_h16=False, bank_a=7, bank_b=5,
                            s16c=lambda c: (merged[c].ap, merged[c].buf))
        A.free(*mine)
        A.lo = mark
        hole.free_all()
        r16.free_all()
        A.free(self.q16, self.k16, self.vaug, self.f16)
        A.hi = A.nbytes
        self.h16 = [A.tile("h16_%d" % k, h16_off[k], BF16, [NCH, blkN(k)]).chunked(NCH) for k in range(NBLK)]
        self.h16_stale = True

    def run_pass(self, b):
        mark0 = self.A.lo
        ffw = self.alloc_ffn_weights() if self.stage >= 1 else None
        self.phase_input(b)
        if self.stage >= 1:
            if self.stage >= 2:
                self.alloc_win()
            self.phase_ffn(b, "ff1", PC_LN1_G, PC_LN1_B, do_output=(self.stage == 1),
                           prefetch=self.load_win if self.stage >= 2 else None, weights=ffw, mark=mark0)
        if self.stage >= 2:
            self.phase_proj(b)
            self.phase_mix(b)
        if self.stage >= 3:
            self.phase_ffn(b, "ff2", PC_LN3_G, PC_LN3_B, do_output=True)
        if self.stage in (0, 2):
            mark = self.A.lo
            ostage = [self.A.low("ost%d" % i, F32, [D]) for i in range(2)]
            for blk in range(4):
                self.output_block(b, blk, ostage)
            self.A.free(*ostage)
            self.A.lo = mark


_CONST_CACHE = {}


def _consts():
    if _CONST_CACHE:
        return _CONST_CACHE
    bf = ml_dtypes.bfloat16
    ident = np.eye(128, dtype=np.float32)
    cm = np.zeros((6, 128, 128), np.float64)
    cm[CM_ONES1024] = 1.0 / 1024
    cm[CM_BD64][0:64, 0:64] = 1.0 / 64
    cm[CM_BD64][64:128, 64:128] = 1.0 / 64
    cm[CM_ONES512] = 1.0 / 512
    for i in range(64):
        cm[CM_ROT][2 * i + 1, 2 * i] = -1.0
        cm[CM_ROT][2 * i, 2 * i + 1] = 1.0
    cc = np.arange(64)
    ang = 2 * np.pi * ((cc[:, None] * cc[None, :]) % 64) / 64.0
    c64 = np.cos(ang) / 8.0
    s64 = np.sin(ang) / 8.0
    for g in range(2):
        cm[CM_BDC][64 * g:64 * g + 64, 64 * g:64 * g + 64] = c64
        cm[CM_BDSN][64 * g:64 * g + 64, 64 * g:64 * g + 64] = -s64
    cmats = np.ascontiguousarray(cm.transpose(1, 0, 2).reshape(128, 6 * 128)).astype(bf)
    t = np.arange(SEQ)
    row = (t // 64).astype(np.float32)
    col = (t % 64).astype(np.float32)
    inv_freq = (np.float32(10000.0) ** (-np.arange(16, dtype=np.float32) / np.float32(16))).astype(np.float32)
    angt = np.concatenate([row[:, None] * inv_freq, col[:, None] * inv_freq], axis=-1).astype(np.float32)
    angt = np.concatenate([angt, np.zeros((NMETA, 32), np.float32)], axis=0)
    pidx = (np.arange(128) % 64) // 2
    cosT = np.cos(angt)[:, pidx].T.astype(np.float32)
    sinT = np.sin(angt)[:, pidx].T.astype(np.float32)
    rope = np.ascontiguousarray(np.concatenate([cosT, sinT], axis=1))
    pos_rows = np.zeros((NKT, 128), np.int64)
    for kt in range(16):
        pos_rows[kt] = NMETA + kt * 128 + np.arange(128)
    pos_rows[16, :NMETA] = np.arange(NMETA)
    dftc = np.zeros((8, 128, NKT, 256), np.float32)
    dfts = np.zeros((8, 128, NKT, 256), np.float32)
    for idx in range(8):
        lp = NMETA + idx * 256 + np.arange(256)
        prod = (pos_rows[:, :, None] * lp[None, None, :]) % LTOT
        a = 2 * np.pi * prod.astype(np.float64) / LTOT
        cmat = (np.cos(a) / np.sqrt(LTOT)).astype(np.float32)
        smat = (np.sin(a) / np.sqrt(LTOT)).astype(np.float32)
        cmat[16, NMETA:, :] = 0
        smat[16, NMETA:, :] = 0
        dftc[idx] = cmat.transpose(1, 0, 2)
        dfts[idx] = smat.transpose(1, 0, 2)
    _CONST_CACHE.update(
        ident=ident, cmats=cmats, rope=rope,
        dftc=np.ascontiguousarray(dftc.reshape(8, 128, NKT * 256)).astype(bf),
        dfts=np.ascontiguousarray(dfts.reshape(8, 128, NKT * 256)).astype(bf))
    return _CONST_CACHE


def _fm(v):
    return np.ascontiguousarray(np.asarray(v, np.float32).reshape(-1, 128).T)


def _pack_params(inp):
    p = np.zeros((128, NPAR), np.float32)
    p[:, PC_EMB_G:PC_EMB_G + 8] = _fm(inp["ln_emb_g"])
    p[:, PC_EMB_B:PC_EMB_B + 8] = _fm(inp["ln_emb_b"])
    p[:, PC_LN1_G:PC_LN1_G + 8] = _fm(inp["ln1_g"][0])
    p[:, PC_LN1_B:PC_LN1_B + 8] = _fm(inp["ln1_b"][0])
    p[:, PC_LN2_G:PC_LN2_G + 8] = _fm(inp["ln2_g"][0])
    p[:, PC_LN2_B:PC_LN2_B + 8] = _fm(inp["ln2_b"][0])
    p[:, PC_LN3_G:PC_LN3_G + 8] = _fm(inp["ln3_g"][0])
    p[:, PC_LN3_B:PC_LN3_B + 8] = _fm(inp["ln3_b"][0])
    p[:, PC_QG] = np.tile(np.asarray(inp["q_norm_g"][0], np.float32), 2)
    p[:, PC_KG] = np.tile(np.asarray(inp["k_norm_g"][0], np.float32), 2)
    p[:, PC_OG:PC_OG + 4] = _fm(inp["attn_out_g"][0])
    p[:, PC_OG + 4:PC_OG + 8] = _fm(inp["fourier_out_g"][0])
    return p


_PROG_CACHE = {}


def _get_prog(npass, stage):
    key = (npass, stage)
    if key not in _PROG_CACHE:
        _PROG_CACHE[key] = Builder(npass, stage).build()
    return _PROG_CACHE[key]


def run(inputs, ncores=8, npass=2, stage=3):
    c = _consts()
    inp = {k: np.asarray(v) for k, v in inputs.items()}
    x = np.ascontiguousarray(inp["x"], dtype=np.float32)
    shared = dict(
        meta=np.ascontiguousarray(inp["meta_tokens"], dtype=np.float32),
        params=_pack_params(inp), ident=c["ident"], cmats=c["cmats"], rope=c["rope"],
        dftc=c["dftc"], dfts=c["dfts"],
        ff1_gate=np.ascontiguousarray(inp["ff1_gate"][0], dtype=np.float32),
        ff1_up=np.ascontiguousarray(inp["ff1_up"][0], dtype=np.float32),
        ff1_down=np.ascontiguousarray(inp["ff1_down"][0], dtype=np.float32),
        ff2_gate=np.ascontiguousarray(inp["ff2_gate"][0], dtype=np.float32),
        ff2_up=np.ascontiguousarray(inp["ff2_up"][0], dtype=np.float32),
        ff2_down=np.ascontiguousarray(inp["ff2_down"][0], dtype=np.float32),
        w_in=np.ascontiguousarray(inp["w_in"][0], dtype=np.float32),
        w_out=np.ascontiguousarray(inp["w_out"][0], dtype=np.float32),
    )
    nc = _get_prog(npass, stage)
    in_maps = []
    for i in range(ncores):
        m = dict(shared)
        m["x"] = np.ascontiguousarray(x[i * npass:(i + 1) * npass])
        in_maps.append(m)
    res = run_bass_kernel_spmd(nc, in_maps, core_ids=list(range(ncores)))
    return np.concatenate([np.asarray(r["out"]) for r in res.results], axis=0)


def kernel(**inputs):
    return run(inputs, ncores=8, npass=2, stage=3).astype(np.float32)
```
